# Optimizing a Trainium2 kernel written in Bass

```python
import jax, jax.numpy as jnp
from jax import lax
import numpy as np

D_MODEL = 1024
BATCH = 4
SEQ = 4096
DEPTH = 4

CHUNK = 64
N_MIXERS = 4
N_HEADS = 16
HEAD_DIM = D_MODEL // N_HEADS
SB_BLOCK = 128
GMLP_BLOCK = 128
GMLP_WIDTH = 2 * D_MODEL
GMLP_GROUPS = 8
CONV_WIDTH = 3
BAND_CHUNKS = 8
REL_CLIP = 128
D_FF = -(-8 * D_MODEL // (3 * 256)) * 256
EPS = 1e-6

kernel_name = "hybrid_chunk_causal_trunk"


def rms_norm(x, g):
    xf = x.astype(jnp.float32)
    y = xf * lax.rsqrt(jnp.mean(xf * xf, axis=-1, keepdims=True) + EPS)
    return (y * g.astype(jnp.float32)).astype(x.dtype)


def layer_norm(x, g):
    xf = x.astype(jnp.float32)
    mu = jnp.mean(xf, axis=-1, keepdims=True)
    xc = xf - mu
    var = jnp.mean(xc * xc, axis=-1, keepdims=True)
    return (xc * lax.rsqrt(var + EPS) * g.astype(jnp.float32)).astype(x.dtype)


def modulate(h, shift, scale):
    return h * (1 + scale) + shift


def stick_breaking_mixer(h, w_qkv, w_o):
    b, s, _ = h.shape
    qkv = (h @ w_qkv).reshape(b, s, 3, N_HEADS, HEAD_DIM)
    q, k, v = qkv[:, :, 0], qkv[:, :, 1], qkv[:, :, 2]
    scale = HEAD_DIM ** -0.5
    outs = []
    for q0 in range(0, s, SB_BLOCK):
        end = q0 + SB_BLOCK
        z = jnp.einsum('bthd,bshd->bhts', q[:, q0:end], k[:, :end]).astype(jnp.float32) * scale
        t_pos = q0 + jnp.arange(SB_BLOCK)[:, None]
        s_pos = jnp.arange(end)[None, :]
        mask = s_pos < t_pos
        sp = jnp.where(mask, jax.nn.softplus(z), 0.0)
        rest = lax.cumsum(sp, axis=3, reverse=True)
        log_a = jnp.where(mask, z - rest, -jnp.inf)
        a = jnp.exp(log_a).astype(v.dtype)
        outs.append(jnp.einsum('bhts,bshd->bthd', a, v[:, :end]))
    o = jnp.concatenate(outs, axis=1).reshape(b, s, D_MODEL)
    return o @ w_o


def spatial_gating_mixer(h, w_in, ln_g, w_s, s_bias, w_out):
    b, s, _ = h.shape
    z = jax.nn.gelu(h @ w_in)
    u, v = jnp.split(z, 2, axis=-1)
    v = layer_norm(v, ln_g)
    nb = s // GMLP_BLOCK
    gc = GMLP_WIDTH // GMLP_GROUPS
    v = v.reshape(b, nb, GMLP_BLOCK, GMLP_GROUPS, gc)
    pos_chunk = jnp.arange(GMLP_BLOCK) // CHUNK
    mask = pos_chunk[:, None] >= pos_chunk[None, :]
    ws = jnp.where(mask[None], w_s, 0.0).astype(v.dtype)
    sv = jnp.einsum('gij,bnjgc->bnigc', ws, v) + s_bias.T[:, :, None].astype(v.dtype)
    y = u * sv.reshape(b, s, GMLP_WIDTH)
    return y @ w_out


def short_conv_mixer(h, w_in, conv_w, w_out):
    gb, gc, xt = jnp.split(h @ w_in, 3, axis=-1)
    y = gc * xt
    yc = lax.conv_general_dilated(
        y, conv_w.reshape(CONV_WIDTH, 1, D_MODEL).astype(y.dtype),
        window_strides=(1,), padding=[(CONV_WIDTH - 1, 0)],
        dimension_numbers=('NWC', 'WIO', 'NWC'), feature_group_count=D_MODEL)
    return (gb * yc) @ w_out


def chunk_band_attention_mixer(h, w_qkv, rel_bias, w_o):
    b, s, _ = h.shape
    nc = s // CHUNK
    nband = BAND_CHUNKS + 1
    qkv = (h @ w_qkv).reshape(b, s, 3, N_HEADS, HEAD_DIM)
    q = qkv[:, :, 0].reshape(b, nc, CHUNK, N_HEADS, HEAD_DIM)
    pad = ((0, 0), (BAND_CHUNKS * CHUNK, 0), (0, 0), (0, 0))
    kp = jnp.pad(qkv[:, :, 1], pad).reshape(b, nc + BAND_CHUNKS, CHUNK, N_HEADS, HEAD_DIM)
    vp = jnp.pad(qkv[:, :, 2], pad).reshape(b, nc + BAND_CHUNKS, CHUNK, N_HEADS, HEAD_DIM)
    scale = HEAD_DIM ** -0.5
    scores = jnp.concatenate(
        [jnp.einsum('bcihd,bcjhd->bhcij', q, kp[:, o:o + nc]) for o in range(nband)],
        axis=-1).astype(jnp.float32) * scale
    i_pos = jnp.arange(CHUNK)[:, None]
    j_pos = jnp.arange(nband * CHUNK)[None, :]
    dist = i_pos + BAND_CHUNKS * CHUNK - j_pos
    idx = jnp.clip(dist, -REL_CLIP, REL_CLIP) + REL_CLIP
    bias = rel_bias[:, idx].astype(jnp.float32)
    key_chunk = jnp.arange(nc)[:, None] - BAND_CHUNKS + j_pos // CHUNK
    valid = (key_chunk >= 0)[:, None, :]
    scores = jnp.where(valid, scores + bias[:, None], -jnp.inf)
    p = jax.nn.softmax(scores, axis=-1).astype(vp.dtype)
    out = jnp.einsum('bhcij,bcjhd->bcihd', p[..., :CHUNK], vp[:, 0:nc])
    for o in range(1, nband):
        out = out + jnp.einsum('bhcij,bcjhd->bcihd', p[..., o * CHUNK:(o + 1) * CHUNK], vp[:, o:o + nc])
    return out.reshape(b, s, D_MODEL) @ w_o


def swiglu_ffn(h, w_in, w_out):
    g, u = jnp.split(h @ w_in, 2, axis=-1)
    return (jax.nn.silu(g) * u) @ w_out


def _n_layers_of(m):
    return len(range(m, DEPTH, N_MIXERS))


def _dense(key, shape, fan_in, gain=1.0):
    return jax.random.normal(key, shape, jnp.float32) * (gain * fan_in ** -0.5)


def setup_inputs(seed: int = 0) -> dict:
    key = jax.random.key(seed)
    ks = jax.random.split(key, 21)
    na, nb, ncv, nd = (_n_layers_of(m) for m in range(N_MIXERS))
    D = D_MODEL
    nrm = lambda k, shape: jax.random.normal(k, shape, jnp.float32)
    return {
        "x": nrm(ks[0], (BATCH, SEQ, D)),
        "c": nrm(ks[1], (BATCH, D)),
        "ada_w": _dense(ks[2], (DEPTH, D, 6 * D), D, 0.5),
        "ada_b": 0.02 * nrm(ks[3], (DEPTH, 6 * D)),
        "norm_g": 1.0 + 0.1 * nrm(ks[4], (DEPTH, 4, D)),
        "ffn_w_in": _dense(ks[5], (DEPTH, D, 2 * D_FF), D),
        "ffn_w_out": _dense(ks[6], (DEPTH, D_FF, D), D_FF),
        "sb_w_qkv": _dense(ks[7], (na, D, 3 * D), D),
        "sb_w_o": _dense(ks[8], (na, D, D), D),
        "sg_w_in": _dense(ks[9], (nb, D, 2 * GMLP_WIDTH), D),
        "sg_ln_g": 1.0 + 0.1 * nrm(ks[10], (nb, GMLP_WIDTH)),
        "sg_w_s": _dense(ks[11], (nb, GMLP_GROUPS, GMLP_BLOCK, GMLP_BLOCK), GMLP_BLOCK),
        "sg_bias": 1.0 + 0.1 * nrm(ks[12], (nb, GMLP_GROUPS, GMLP_BLOCK)),
        "sg_w_out": _dense(ks[13], (nb, GMLP_WIDTH, D), GMLP_WIDTH),
        "sc_w_in": _dense(ks[14], (ncv, D, 3 * D), D),
        "sc_conv_w": _dense(ks[15], (ncv, CONV_WIDTH, D), CONV_WIDTH),
        "sc_w_out": _dense(ks[16], (ncv, D, D), D),
        "cb_w_qkv": _dense(ks[17], (nd, D, 3 * D), D),
        "cb_rel_bias": 0.5 * nrm(ks[18], (nd, N_HEADS, 2 * REL_CLIP + 1)),
        "cb_w_o": _dense(ks[19], (nd, D, D), D),
    }


def reference(x, c, ada_w, ada_b, norm_g, ffn_w_in, ffn_w_out,
              sb_w_qkv, sb_w_o,
              sg_w_in, sg_ln_g, sg_w_s, sg_bias, sg_w_out,
              sc_w_in, sc_conv_w, sc_w_out,
              cb_w_qkv, cb_rel_bias, cb_w_o):
    mod_all = jnp.einsum('bd,lde->lbe', jax.nn.silu(c), ada_w) + ada_b[:, None]
    for i in range(DEPTH):
        m, r = i % N_MIXERS, i // N_MIXERS
        sh_m, sc_m, gt_m, sh_f, sc_f, gt_f = jnp.split(mod_all[i][:, None, :], 6, axis=-1)
        h = modulate(rms_norm(x, norm_g[i, 0]), sh_m, sc_m)
        if m == 0:
            y = stick_breaking_mixer(h, sb_w_qkv[r], sb_w_o[r])
        elif m == 1:
            y = spatial_gating_mixer(h, sg_w_in[r], sg_ln_g[r], sg_w_s[r], sg_bias[r], sg_w_out[r])
        elif m == 2:
            y = short_conv_mixer(h, sc_w_in[r], sc_conv_w[r], sc_w_out[r])
        else:
            y = chunk_band_attention_mixer(h, cb_w_qkv[r], cb_rel_bias[r], cb_w_o[r])
        x = x + gt_m * rms_norm(y, norm_g[i, 1])
        h = modulate(rms_norm(x, norm_g[i, 2]), sh_f, sc_f)
        y = swiglu_ffn(h, ffn_w_in[i], ffn_w_out[i])
        x = x + gt_f * rms_norm(y, norm_g[i, 3])
    return x
```

```python
import numpy as np
from contextlib import ExitStack
import concourse.bass as bass
import concourse.mybir as mybir
from concourse.bass_utils import run_bass_kernel_spmd

F32 = mybir.dt.float32
BF16 = mybir.dt.bfloat16
AF = mybir.ActivationFunctionType
ALU = mybir.AluOpType

D = 1024
W = 2688
NBLK = 21
NST = 3
SB = 7
ST = 896
TT = 448
KS = 4096
HALO = 1408
DFF = 2816
N_DMA_SLOTS = 24
EPS = 1e-6


class Buf:
    __slots__ = ("name", "last_w", "readers")

    def __init__(self, name):
        self.name = name
        self.last_w = None
        self.readers = {}


class Tracker:
    ENG = ("pe", "act", "dve", "pool", "sp")

    def __init__(self):
        self.prog = {e: [] for e in self.ENG}
        self.count = {e: 0 for e in self.ENG}
        self.unit = {e: 1 for e in self.ENG}
        for k in range(N_DMA_SLOTS):
            self.count["dma%d" % k] = 0
            self.unit["dma%d" % k] = 16
        self.waited = {e: {} for e in self.ENG}
        self.next_slot = {"sp": 0, "pool": 0}
        self.nbuf = 0

    def buf(self, name=None):
        self.nbuf += 1
        return Buf(name or ("b%d" % self.nbuf))

    def bufs(self, n, name="b"):
        return [self.buf("%s%d" % (name, i)) for i in range(n)]

    def _need(self, eng, dep):
        if dep is None:
            return
        p, idx = dep
        if p == eng and eng == "pe":
            return
        if self.waited[eng].get(p, 0) >= idx:
            return
        self.waited[eng][p] = idx
        self.prog[eng].append(("wait", p, idx * self.unit[p]))

    def _deps(self, eng, reads, writes):
        for b in reads:
            self._need(eng, b.last_w)
        for b in writes:
            self._need(eng, b.last_w)
            for p, idx in list(b.readers.items()):
                self._need(eng, (p, idx))

    @staticmethod
    def _flat(bs):
        out = []
        for b in bs:
            if isinstance(b, (list, tuple)):
                out.extend(Tracker._flat(b))
            else:
                out.append(b)
        return out

    def op(self, eng, fn, reads=(), writes=()):
        reads, writes = self._flat(reads), self._flat(writes)
        self._deps(eng, reads, writes)
        self.count[eng] += 1
        idx = self.count[eng]
        self.prog[eng].append(("op", fn, eng))
        for b in reads:
            b.readers[eng] = idx
        for b in writes:
            b.last_w = (eng, idx)
            b.readers = {}
        return idx

    def dma(self, queue, fn, reads=(), writes=()):
        reads, writes = self._flat(reads), self._flat(writes)
        half = N_DMA_SLOTS // 2
        base = 0 if queue == "sp" else half
        k = base + self.next_slot[queue]
        self.next_slot[queue] = (self.next_slot[queue] + 1) % half
        slot = "dma%d" % k
        if self.count[slot] > 0:
            self._need(queue, (slot, self.count[slot]))
        self._deps(queue, reads, writes)
        self.count[slot] += 1
        idx = self.count[slot]
        self.prog[queue].append(("dma", fn, slot))
        for b in reads:
            b.readers[slot] = idx
        for b in writes:
            b.last_w = (slot, idx)
            b.readers = {}

    def barrier(self):
        prods = list(self.count.keys())
        for eng in self.ENG:
            for p in prods:
                if self.count[p] > 0:
                    self._need(eng, (p, self.count[p]))

    def final_wait(self, eng, bufs):
        for b in bufs:
            self._need(eng, b.last_w)

    def emit(self, nc, sems):
        def run(engname, engine):
            for item in self.prog[engname]:
                if item[0] == "wait":
                    engine.wait_ge(sems[item[1]], item[2])
                elif item[0] == "op":
                    item[1](engine).then_inc(sems[item[2]], 1)
                else:
                    item[1](engine).then_inc(sems[item[2]], 16)
        with nc.Block() as block:
            @block.sync
            def _(e):
                run("sp", e)

            @block.tensor
            def _(e):
                run("pe", e)

            @block.scalar
            def _(e):
                run("act", e)

            @block.vector
            def _(e):
                run("dve", e)

            @block.gpsimd
            def _(e):
                run("pool", e)


class Ctx:
    pass


def build(layers):
    nc = bass.Bass("TRN2", target_bir_lowering=False)
    tr = Tracker()
    K = Ctx()
    K.nc, K.tr = nc, tr
    has0 = 0 in layers

    def din(name, shape):
        return nc.dram_tensor(name, list(shape), F32, kind="ExternalInput").ap()

    dr = {}
    if has0:
        dr["xseq"] = din("xseq", [KS, D])
        dr["valid"] = din("valid", [128, 32])
        dr["sb_w_qkv"] = din("sb_w_qkv", [D, 3 * D])
        dr["sb_w_o"] = din("sb_w_o", [D, D])
        dr["sb_mask"] = din("sb_mask", [128, 3, 384])
        dr["sb_tri"] = din("sb_tri", [128, 128])
    else:
        dr["xin"] = din("xin", [W, D])
    dr["cfm"] = din("cfm", [128, 8])
    dr["ident"] = din("ident", [128, 128])
    dr["ngfm"] = din("ngfm", [128, 128])
    dr["ngrow"] = din("ngrow", [1, 16 * D])
    dr["ada_w"] = din("ada_w", [len(layers), D, 6 * D])
    dr["ada_b"] = din("ada_b", [1, 4 * 6 * D])
    dr["ffn_w_in"] = din("ffn_w_in", [len(layers), D, 2 * DFF])
    dr["ffn_w_out"] = din("ffn_w_out", [len(layers), DFF, D])
    if 1 in layers:
        dr["sg_w_in"] = din("sg_w_in", [D, 4096])
        dr["sg_lng"] = din("sg_lng", [1, 2048])
        dr["sg_wsT"] = din("sg_wsT", [128, 8, 128])
        dr["sg_m01"] = din("sg_m01", [128, 128])
        dr["sg_bias"] = din("sg_bias", [1, 1024])
        dr["sg_w_out"] = din("sg_w_out", [2048, D])
    if 2 in layers:
        dr["sc_w_in"] = din("sc_w_in", [D, 3 * D])
        dr["sc_cw"] = din("sc_cw", [128, 24])
        dr["sc_w_out"] = din("sc_w_out", [D, D])
    if 3 in layers:
        dr["cb_w_qkv"] = din("cb_w_qkv", [D, 3 * D])
        dr["cb_w_o"] = din("cb_w_o", [D, D])
        dr["cb_bt"] = din("cb_bt", [128, 16, 2, 128])
        dr["cb_bfar"] = din("cb_bfar", [128, 16])
        dr["cb_m04"] = din("cb_m04", [128, 2, 128])
    xout = nc.dram_tensor("xout", [W, D], F32, kind="ExternalOutput").ap()
    xs = nc.dram_tensor("xs", [W, D], F32, kind="Internal").ap()
    K.dr = dr
    K.lpos = {l: i for i, l in enumerate(layers)}

    with ExitStack() as es:
        def sb(name, shape, dt=F32):
            return es.enter_context(nc.sbuf_tensor("sb_" + name, list(shape), dt))

        def ps(name, shape, dt=F32):
            return es.enter_context(nc.psum_tensor("ps_" + name, list(shape), dt))
        K.sb, K.ps = sb, ps
        sems = {}
        for e in list(Tracker.ENG) + ["dma%d" % k for k in range(N_DMA_SLOTS)]:
            sems[e] = es.enter_context(nc.semaphore("sem_" + e))

        K.py = ps("py", [128, 1024])
        K.pab = ps("pab", [128, 4, 512])
        K.pm2 = ps("pm2", [128, 2, 512])
        K.pT = K.pm2[:, 0, :].bitcast(BF16).rearrange("p (c i) -> p c i", i=128)
        K.pm = K.pm2[:, 1, :]
        K.PY = tr.buf("PY")
        K.PAB = tr.bufs(4, "PAB")
        K.PT = tr.buf("PT")
        K.PM = tr.buf("PM")

        K.idf = sb("idf", [128, 128])
        K.idb = sb("idb", [128, 128], BF16)
        K.ones_f = sb("ones_f", [128, 128])
        K.ones_b = sb("ones_b", [128, 128], BF16)
        K.epsT = sb("epsT", [128, 1])
        K.ngfm = sb("ngfm", [128, 128])
        K.cfm = sb("cfm", [128, 8])
        K.csil = sb("csil", [128, 8], BF16)
        K.CONST = tr.buf("CONST")
        C = K.CONST
        tr.dma("sp", lambda e: e.dma_start(out=K.idf[:], in_=dr["ident"]), writes=[C])
        tr.dma("sp", lambda e: e.dma_start(out=K.ngfm[:], in_=dr["ngfm"]), writes=[C])
        tr.dma("sp", lambda e: e.dma_start(out=K.cfm[:], in_=dr["cfm"]), writes=[C])
        tr.op("dve", lambda e: e.tensor_copy(out=K.idb[:], in_=K.idf[:]), reads=[C], writes=[C])
        tr.op("dve", lambda e: e.memset(K.ones_f[:], 1.0), writes=[C])
        tr.op("dve", lambda e: e.memset(K.ones_b[:], 1.0), writes=[C])
        tr.op("dve", lambda e: e.memset(K.epsT[:], EPS), writes=[C])
        tr.op("act", lambda e: e.activation(out=K.csil[:], in_=K.cfm[:], func=AF.Silu), reads=[C], writes=[C])

        K.xb = [sb("xb%d" % i, [128, D]) for i in range(2)]
        K.XB = tr.bufs(2, "XB")
        K.xrot = 0
        K.junk = sb("junk", [128, D], BF16)
        K.JUNK = tr.buf("JUNK")
        K.xn2 = [sb("xn%d" % i, [128, D], BF16) for i in range(2)]
        K.XN2 = tr.bufs(2, "XN")
        K.xnrot = 0
        K.st4 = [sb("st4_%d" % i, [128, 4]) for i in range(2)]
        K.ST4 = tr.bufs(2, "ST4")
        K.strot = 0
        K.hT = sb("hT", [128, 8, 1024], BF16)
        K.HT = tr.bufs(8, "HT")
        K.tmpf = sb("tmpf", [128, D])
        K.TMPF = tr.buf("TMPF")
        K.modc_all = sb("modc", [128, 4, 6, 8])
        K.MODCL = tr.bufs(4, "MODC")
        K.gsc = nc.dram_tensor("gsc", [4, 2, D], F32, kind="Internal").ap()
        K.GSC = [tr.bufs(2, "GSC%d_" % l_) for l_ in range(4)]
        K.bg = []
        K.gg = [sb("gg%d" % i, [128, D]) for i in range(2)]
        K.GG = tr.bufs(2, "GG")
        K.row = sb("row", [1, 512])
        K.ROW = tr.buf("ROW")
        K.row2 = sb("row2", [1, 512])
        K.ROW2 = tr.buf("ROW2")
        K.wstr = sb("wstr", [128, 8, 1024], BF16)
        K.WSTR = tr.bufs(8, "WSTR")
        K.wrot = {128: 0, 256: 0, 512: 0}
        K.wbig = sb("wbig", [128, 22, D], BF16)
        K.WBIG = tr.buf("WBIG")
        K.act = sb("actT", [128, 24, ST], BF16)
        K.ACT = tr.bufs(24, "ACT")
        K.sg = [sb("sgt%d" % i, [128, 512]) for i in range(2)]
        K.arena = {F32: sb("arena_f", [128, 2304]), BF16: sb("arena_b", [128, 21120], BF16)}
        K.aoff = {F32: 0, BF16: 0}

        def larena(name, shape, dt=F32):
            n = 1
            for d_ in shape[1:]:
                n *= d_
            off = K.aoff[dt]
            K.aoff[dt] = off + n
            t = K.arena[dt]
            assert off + n <= (2304 if dt == F32 else 21120), (name, off, n)
            ap = t[0:shape[0], off:off + n]
            if len(shape) == 3:
                ap = ap.rearrange("p (a b) -> p a b", a=shape[1], b=shape[2])
            elif len(shape) == 4:
                ap = ap.rearrange("p (a b c) -> p a b c", a=shape[1], b=shape[2], c=shape[3])
            return ap

        def lreset():
            tr.barrier()
            K.aoff = {F32: 0, BF16: 0}
        K.larena, K.lreset = larena, lreset
        K.SG = tr.bufs(2, "SG")
        K.sgrot = 0
        K.XS = tr.bufs(NBLK, "XS")
        K.XO = tr.bufs(NBLK, "XO")

        src = None
        n_sub = 2 * len(layers)
        sub = 0
        for t_ in adaln_tasks(K, layers[0]):
            t_()
        for l in layers[1:]:
            K.bg.extend(adaln_tasks(K, l))
        if not has0:
            while K.bg:
                K.bg.pop(0)()
        for l in layers:
            adaln_fetch(K, l)
            dst = xout if sub == n_sub - 1 else xs
            dstB = K.XO if sub == n_sub - 1 else K.XS
            if l == 0:
                mixer0(K, l, dst, dstB)
            else:
                s_ap, s_B = (dr["xin"], None) if sub == 0 else (xs, K.XS)
                [None, mixer1, mixer2, mixer3][l](K, l, s_ap, s_B, dst, dstB)
            sub += 1
            dst = xout if sub == n_sub - 1 else xs
            dstB = K.XO if sub == n_sub - 1 else K.XS
            ffn(K, l, xs, K.XS, dst, dstB)
            sub += 1
        tr.final_wait("sp", K.XO)
        tr.emit(nc, sems)
    return nc


def mm(K, groups, reads, writes):
    def fn(e):
        ins = None
        for out_ap, pairs in groups:
            n = len(pairs)
            for i, (l, r) in enumerate(pairs):
                ins = e.matmul(out_ap, lhsT=l, rhs=r, start=(i == 0), stop=(i == n - 1))
        return ins
    K.tr.op("pe", fn, reads=reads, writes=writes)


def wload(K, dram_ap, ncols, kchunks=8):
    nslots = 1024 // ncols
    i = K.wrot[ncols]
    K.wrot[ncols] = (i + 1) % nslots
    view = K.wstr[:, 0:kchunks, i * ncols:(i + 1) * ncols]
    u = ncols // 128
    bufs = K.WSTR[i * u:(i + 1) * u]
    K.tr.dma("pool", lambda e: e.dma_start(out=view, in_=dram_ap.rearrange("(c p) n -> p c n", p=128)),
             writes=bufs)
    return view, bufs


def load_big(K, dram_ap, kchunks):
    step = 4
    for k0 in range(0, kchunks, step):
        k1 = min(kchunks, k0 + step)
        v = K.wbig[:, k0:k1, :]
        src = dram_ap[k0 * 128:k1 * 128, :].rearrange("(c p) n -> p c n", p=128)
        K.tr.dma("pool", (lambda v=v, src=src: (lambda e: e.dma_start(out=v, in_=src)))(), writes=[K.WBIG])


def adaln_tasks(K, l):
    tr, dr = K.tr, K.dr
    g_m = (l * 4 + 1) * D
    g_f = (l * 4 + 3) * D
    MC = K.MODCL[l]
    tasks = []

    def block(j):
        vec, half = j // 2, j % 2
        wt, WB = wload(K, dr["ada_w"][K.lpos[l], :, j * 512:(j + 1) * 512], 512)
        mm(K, [(K.pm[0:1, :], [(K.csil[:, k:k + 1], wt[:, k, :]) for k in range(8)])],
           reads=[K.CONST, WB], writes=[K.PM])
        off = l * 6 * D + j * 512
        tr.dma("sp", lambda e: e.dma_start(out=K.row[:], in_=dr["ada_b"][0:1, off:off + 512]), writes=[K.ROW])
        tr.op("dve", lambda e: e.tensor_tensor(out=K.row[:], in0=K.pm[0:1, :], in1=K.row[:], op=ALU.add),
              reads=[K.PM, K.ROW], writes=[K.ROW])
        if vec in (2, 5):
            goff = (g_m if vec == 2 else g_f) + half * 512
            gi = 0 if vec == 2 else 1
            tr.dma("sp", lambda e: e.dma_start(out=K.row2[0:1, 0:512], in_=dr["ngrow"][0:1, goff:goff + 512]), writes=[K.ROW2])
            tr.op("dve", lambda e: e.tensor_tensor(out=K.row[:], in0=K.row[:], in1=K.row2[0:1, 0:512], op=ALU.mult),
                  reads=[K.ROW, K.ROW2], writes=[K.ROW])
            tr.dma("sp", lambda e: e.dma_start(out=K.gsc[l, gi:gi + 1, half * 512:(half + 1) * 512], in_=K.row[0:1, :]),
                   reads=[K.ROW], writes=[K.GSC[l][gi]])
        else:
            ci = {0: 0, 1: 1, 3: 2, 4: 3}[vec]
            mm(K, [(K.pm[:, q:q + 1], [(K.row[0:1, q * 128:(q + 1) * 128], K.ones_f[0:1, 0:1])]) for q in range(4)],
               reads=[K.CONST, K.ROW], writes=[K.PM])
            tr.op("act", lambda e: e.copy(out=K.modc_all[:, l, ci, half * 4:half * 4 + 4], in_=K.pm[:, 0:4]),
                  reads=[K.PM], writes=[MC])

    def finish():
        for (ci, ao, k) in ((1, 4, 0), (3, 5, 2)):
            gcol = (l * 4 + k) * 8
            tr.op("dve", (lambda ci=ci, ao=ao: (lambda e: e.tensor_scalar(
                out=K.modc_all[:, l, ao, :], in0=K.modc_all[:, l, ci, :], scalar1=1.0, scalar2=None, op0=ALU.add)))(),
                reads=[MC], writes=[MC])
            tr.op("dve", (lambda ao=ao, gcol=gcol: (lambda e: e.tensor_tensor(
                out=K.modc_all[:, l, ao, :], in0=K.modc_all[:, l, ao, :], in1=K.ngfm[:, gcol:gcol + 8], op=ALU.mult)))(),
                reads=[MC, K.CONST], writes=[MC])

    for j in range(12):
        tasks.append((lambda j=j: block(j)))
    tasks.append(finish)
    return tasks


def adaln_fetch(K, l):
    tr = K.tr
    K.modc = K.modc_all[:, l]
    K.MODC = K.MODCL[l]
    for gi in range(2):
        tr.dma("sp", (lambda gi=gi: (lambda e: e.dma_start(out=K.gg[gi][:], in_=K.gsc[l, gi:gi + 1, :].to_broadcast([128, D]))))(),
               reads=[K.GSC[l][gi]], writes=[K.GG[gi]])


def load_x(K, src_ap, srcB, blk, row0=None):
    i = K.xrot
    K.xrot = (i + 1) % 2
    r0 = blk * 128 if row0 is None else row0
    t = K.xb[i]
    K.tr.dma("sp", lambda e: e.dma_start(out=t[:], in_=src_ap[r0:r0 + 128, :]),
             reads=([srcB[blk]] if srcB is not None else []), writes=[K.XB[i]])
    return t, K.XB[i]


def rstd_of(K, in_ap, inB, n, mean_col=None, extra_reads=()):
    tr = K.tr
    i = K.strot
    K.strot = (i + 1) % 2
    st, STB = K.st4[i], K.ST4[i]
    sc = float(n) ** -0.5

    tr.op("act", lambda e: e.activation(out=K.junk[:, 0:n], in_=in_ap, func=AF.Square, scale=sc, accum_out=st[:, 0:1]),
          reads=[inB] + list(extra_reads), writes=[K.JUNK, STB])
    tr.op("act", lambda e: e.activation(out=st[:, 1:2], in_=st[:, 0:1], func=AF.Sqrt, bias=K.epsT[:, 0:1], scale=1.0),
          reads=[STB, K.CONST], writes=[STB])
    tr.op("dve", lambda e: e.reciprocal(out=st[:, 2:3], in_=st[:, 1:2]), reads=[STB], writes=[STB])
    return st, STB


def norm_in_seq(K, items, aidx, hT=None):
    for st_ in norm_in_steps(K, items, aidx, hT):
        st_()


def norm_in_steps(K, items, aidx, hT=None):
    tr = K.tr
    sh = 0 if aidx == 4 else 2
    n = len(items)
    xns = []
    modc, MODC = K.modc, K.MODC
    hT = K.hT if hT is None else hT

    def stage1(i):
        load_fn, hcol, HB = items[i]
        xt, XB = load_fn()
        st, STB = rstd_of(K, xt[:], XB, D)
        j = K.xnrot
        K.xnrot = 1 - j
        xn, XNB = K.xn2[j], K.XN2[j]
        tr.op("dve", lambda e: e.tensor_scalar(out=xn[:], in0=xt[:], scalar1=st[:, 2:3], scalar2=None, op0=ALU.mult),
              reads=[XB, STB], writes=[XNB])
        xns.append((xn, XNB))

    def stage2(i):
        load_fn, hcol, HB = items[i]
        xn, XNB = xns[i]

        def tps(e):
            for c in range(8):
                ins = e.transpose(out=K.pT[:, c, :], in_=xn[:, c * 128:(c + 1) * 128], identity=K.idb[:])
            return ins
        tr.op("pe", tps, reads=[XNB, K.CONST], writes=[K.PT])

        def mod_a(e):
            for c in range(0, 5):
                ins = e.activation(out=hT[:, c, hcol:hcol + 128], in_=K.pT[:, c, :], func=AF.Identity,
                                   scale=modc[:, aidx, c:c + 1], bias=modc[:, sh, c:c + 1])
            return ins

        def mod_d(e):
            for c in range(5, 8):
                ins = e.tensor_scalar(out=hT[:, c, hcol:hcol + 128], in0=K.pT[:, c, :], scalar1=modc[:, aidx, c:c + 1],
                                      scalar2=modc[:, sh, c:c + 1], op0=ALU.mult, op1=ALU.add)
            return ins
        tr.op("act", mod_a, reads=[K.PT, MODC], writes=[HB])
        tr.op("dve", mod_d, reads=[K.PT, MODC], writes=[HB])

    def mk(i):
        def step():
            if i < n:
                stage1(i)
            if i >= 1:
                stage2(i - 1)
        return step
    return [mk(i) for i in range(n + 1)]


def norm_out(K, xt, XB, gi, dst_ap, dstB, blk):
    tr = K.tr
    pyv, PYB = K.cur_py
    st, STB = rstd_of(K, pyv, PYB[0], D, extra_reads=PYB[1:])
    tr.op("dve", lambda e: e.tensor_tensor(out=K.tmpf[:], in0=pyv, in1=K.gg[gi][:], op=ALU.mult),
          reads=list(PYB) + [K.GG[gi]], writes=[K.TMPF])
    tr.op("dve", lambda e: e.scalar_tensor_tensor(out=xt[:], in0=K.tmpf[:], scalar=st[:, 2:3], in1=xt[:],
                                                   op0=ALU.mult, op1=ALU.add),
          reads=[K.TMPF, STB, XB], writes=[XB])
    tr.dma("sp", lambda e: e.dma_start(out=dst_ap[blk * 128:(blk + 1) * 128, :], in_=xt[:]),
           reads=[XB], writes=[dstB[blk]])


def final_proj(K, actT, ACTB, kchunks, bcol, reads_extra=(), wsel=None):
    K.pyrot = 1 - getattr(K, "pyrot", 1)
    if K.pyrot == 0:
        pyv, PYB = K.py[:, :], [K.PY]
    else:
        pyv, PYB = K.pab[:, 0:2, :].rearrange("p a b -> p (a b)"), K.PAB[0:2]
    K.cur_py = (pyv, PYB)
    groups = []
    for half in range(2):
        groups.append((pyv[:, half * 512:(half + 1) * 512],
                       [(actT[:, k, bcol:bcol + 128], (wsel(k) if wsel else K.wbig[:, k, :])[:, half * 512:(half + 1) * 512])
                        for k in range(kchunks)]))
    mm(K, groups, reads=list(ACTB) + [K.WBIG] + list(reads_extra), writes=PYB)


def ffn(K, l, src_ap, srcB, dst_ap, dstB):
    tr, dr = K.tr, K.dr
    K.lreset()
    hTb = K.larena("hTb", [128, 8, 1024], BF16)
    HTb = tr.bufs(8, "HTb")
    hbuf = [(K.hT, K.HT), (hTb, HTb)]
    load_big(K, dr["ffn_w_out"][K.lpos[l]], 22)
    win = dr["ffn_w_in"][K.lpos[l]]

    def nsteps(s):
        hT, HB = hbuf[s % 2]
        return norm_in_steps(K, [((lambda blk=s * SB + b: load_x(K, src_ap, srcB, blk)), b * 128, HB[b]) for b in range(SB)], 5, hT)

    for st_ in nsteps(0):
        st_()
    for s in range(NST):
        hT, HB = hbuf[s % 2]
        nxt = nsteps(s + 1) if s + 1 < NST else []
        for pc in range(11):
            wt, WB = wload(K, win[:, pc * 256:(pc + 1) * 256], 256)
            wu, WU = wload(K, win[:, DFF + pc * 256:DFF + (pc + 1) * 256], 256)
            for fc in range(2):
                f = pc * 2 + fc
                for tt in range(2):
                    pbank = (fc * 2 + tt) % 2 * 2
                    pg, pu = K.pab[:, pbank, 0:TT], K.pab[:, pbank + 1, 0:TT]
                    hs = slice(tt * TT, (tt + 1) * TT)
                    mm(K, [(pg, [(wt[:, k, fc * 128:(fc + 1) * 128], hT[:, k, hs]) for k in range(8)]),
                           (pu, [(wu[:, k, fc * 128:(fc + 1) * 128], hT[:, k, hs]) for k in range(8)])],
                       reads=[WB, WU] + HB[0:SB], writes=[K.PAB[pbank], K.PAB[pbank + 1]])
                    i = K.sgrot
                    K.sgrot = (i + 1) % 2
                    sgt, SGB = K.sg[i], K.SG[i]
                    tr.op("act", (lambda sgt=sgt, pg=pg: (lambda e: e.activation(out=sgt[:, 0:TT], in_=pg, func=AF.Silu)))(),
                          reads=[K.PAB[pbank]], writes=[SGB])
                    tr.op("dve", (lambda sgt=sgt, pu=pu, f=f, hs=hs: (lambda e: e.tensor_tensor(
                        out=K.act[:, f, hs], in0=pu, in1=sgt[:, 0:TT], op=ALU.mult)))(),
                        reads=[K.PAB[pbank + 1], SGB], writes=[K.ACT[f]])
            if 1 <= pc and pc - 1 < len(nxt):
                nxt[pc - 1]()
        for b in range(SB):
            blk = s * SB + b
            final_proj(K, K.act, K.ACT, 22, b * 128)
            xt, XB = load_x(K, src_ap, srcB, blk)
            norm_out(K, xt, XB, 1, dst_ap, dstB, blk)


def mixer2(K, l, src_ap, srcB, dst_ap, dstB):
    tr, dr, sb = K.tr, K.dr, K.larena
    K.lreset()
    if True:
        K.cw = sb("cw", [128, 24])
        K.ybuf = [sb("ybuf%d" % i, [128, 2 + TT]) for i in range(2)]
        K.YBUF = tr.bufs(2, "YBUF")
        K.carry = sb("carry", [128, 8, 2])
        K.CARRY = tr.buf("CARRY")
        K.cacc = sb("cacc", [128, TT])
        K.CACC = tr.buf("CACC")
    tr.dma("sp", lambda e: e.dma_start(out=K.cw[:], in_=dr["sc_cw"]), writes=[K.CONST])
    tr.op("dve", lambda e: e.memset(K.carry[:], 0.0), writes=[K.CARRY])
    load_big(K, dr["sc_w_out"], 8)
    win = dr["sc_w_in"]
    yrot = 0
    for s in range(NST):
        norm_in_seq(K, [((lambda blk=s * SB + b: load_x(K, src_ap, srcB, blk)), b * 128, K.HT[b]) for b in range(SB)], 4)
        for c in range(8):
            wt, WB = wload(K, win[:, c * 128:(c + 1) * 128], 128)
            w2_, WB2 = None, None
            wgc, WGC = wload(K, win[:, D + c * 128:D + (c + 1) * 128], 128)
            wxt, WXT = None, None
            for tt in range(2):
                hs = slice(tt * TT, (tt + 1) * TT)
                if tt == 0:
                    wxt, WXT = wload_extra(K, win[:, 2 * D + c * 128:2 * D + (c + 1) * 128])
                if (c * 2 + tt) % 2 == 0:
                    pgb, pgc, pxt = K.pab[:, 0, 0:TT], K.pab[:, 1, 0:TT], K.pab[:, 2, 0:TT]
                    PGB, PGC, PXT = K.PAB[0], K.PAB[1], K.PAB[2]
                else:
                    pgb, pgc, pxt = K.pab[:, 3, 0:TT], K.py[:, 0:TT], K.py[:, 512:512 + TT]
                    PGB, PGC, PXT = K.PAB[3], K.PY, K.PY
                mm(K, [(pgb, [(wt[:, k, 0:128], K.hT[:, k, hs]) for k in range(8)]),
                       (pgc, [(wgc[:, k, 0:128], K.hT[:, k, hs]) for k in range(8)]),
                       (pxt, [(wxt[:, k, 0:128], K.hT[:, k, hs]) for k in range(8)])],
                   reads=[WB, WGC, WXT] + K.HT[0:SB], writes=[PGB, PGC, PXT])
                yb, YB = K.ybuf[yrot], K.YBUF[yrot]
                yrot = 1 - yrot
                i = K.sgrot
                K.sgrot = (i + 1) % 2
                sgt, SGB = K.sg[i], K.SG[i]
                tr.op("act", (lambda sgt=sgt, pgc=pgc: (lambda e: e.copy(out=sgt[:, 0:TT], in_=pgc)))(),
                      reads=[PGC], writes=[SGB])

                CW = K.CONST
                steps = [
                    (lambda e, yb=yb, c=c: e.tensor_copy(out=yb[:, 0:2], in_=K.carry[:, c, :]), [K.CARRY], [YB]),
                    (lambda e, yb=yb, pxt=pxt, sgt=sgt: e.tensor_tensor(out=yb[:, 2:2 + TT], in0=pxt, in1=sgt[:, 0:TT], op=ALU.mult),
                     [PXT, SGB], [YB]),
                    (lambda e, yb=yb, c=c: e.tensor_copy(out=K.carry[:, c, :], in_=yb[:, TT:TT + 2]), [YB], [K.CARRY]),
                    (lambda e, yb=yb, c=c: e.tensor_scalar(out=K.cacc[:], in0=yb[:, 0:TT], scalar1=K.cw[:, c:c + 1],
                                                          scalar2=None, op0=ALU.mult), [YB, CW], [K.CACC]),
                    (lambda e, yb=yb, c=c: e.scalar_tensor_tensor(out=K.cacc[:], in0=yb[:, 1:1 + TT], scalar=K.cw[:, 8 + c:9 + c],
                                                                 in1=K.cacc[:], op0=ALU.mult, op1=ALU.add), [YB, CW, K.CACC], [K.CACC]),
                    (lambda e, yb=yb, c=c: e.scalar_tensor_tensor(out=K.cacc[:], in0=yb[:, 2:2 + TT], scalar=K.cw[:, 16 + c:17 + c],
                                                                 in1=K.cacc[:], op0=ALU.mult, op1=ALU.add), [YB, CW, K.CACC], [K.CACC]),
                    (lambda e, pgb=pgb, c=c, hs=hs: e.tensor_tensor(out=K.act[:, c, hs], in0=pgb, in1=K.cacc[:], op=ALU.mult),
                     [PGB, K.CACC], [K.ACT[c]]),
                ]
                for fn, rd, wr in steps:
                    tr.op("dve", fn, reads=rd, writes=wr)
        for b in range(SB):
            blk = s * SB + b
            final_proj(K, K.act, K.ACT[0:8], 8, b * 128)
            xt, XB = load_x(K, src_ap, srcB, blk)
            norm_out(K, xt, XB, 0, dst_ap, dstB, blk)


def wload_extra(K, dram_ap):
    return wload(K, dram_ap, 128)


def mixer0(K, l, dst_ap, dstB):
    tr, dr, sb, nc = K.tr, K.dr, K.larena, K.nc
    K.lreset()
    kT_d = nc.dram_tensor("kT_d", [8, 128, KS], BF16, kind="Internal").ap()
    qT_d = nc.dram_tensor("qT_d", [8, 128, KS], BF16, kind="Internal").ap()
    v_d = nc.dram_tensor("v_d", [8, KS, 128], BF16, kind="Internal").ap()
    KD, QD, VD = tr.bufs(8, "KD"), tr.bufs(8, "QD"), tr.bufs(8, "VD")
    valid = sb("valid", [128, 32])
    maskb = sb("maskb", [128, 3, 384], BF16)
    ntri = sb("ntri", [128, 128], BF16)
    nones = sb("nones", [128, 128], BF16)
    trif = sb("trif", [128, 128])
    kTa = sb("kTa", [128, KS], BF16)
    qTa = [sb("qTa%d" % i, [128, W], BF16) for i in range(2)]
    vca = sb("vca", [128, 32, 128], BF16)
    esb = sb("esb", [128, 2, 384])
    Sf = sb("Sf", [128, 2, 384])
    mark_b = K.aoff[BF16]
    qst = sb("qst", [128, 1024], BF16)
    kst = sb("kst", [128, 1024], BF16)
    vst = sb("vst", [128, 8, 128], BF16)
    K.aoff[BF16] = mark_b
    spb = [sb("spb%d" % i, [128, 2, 384], BF16) for i in range(3)]
    abb = [sb("abb%d" % i, [128, 2, 384], BF16) for i in range(2)]
    Sb = [sb("Sb%d" % i, [128, 2, 384], BF16) for i in range(3)]
    G0 = tr.buf("G0")
    QST, KST, VST, KTA, VCA = (tr.buf(n) for n in ("QST", "KST", "VST", "KTA", "VCA"))
    ESB, SF = tr.buf("ESB"), tr.buf("SF")
    SPB, ABB, SBB = tr.bufs(3, "SPB"), tr.bufs(2, "ABB"), tr.bufs(3, "SBB")
    tr.dma("sp", lambda e: e.dma_start(out=valid[:], in_=dr["valid"]), writes=[G0])
    tr.dma("sp", lambda e: e.dma_start(out=trif[:], in_=dr["sb_tri"]), writes=[G0])
    for j in range(3):
        tr.dma("sp", (lambda j=j: (lambda e: e.dma_start(out=K.tmpf[:, 0:384], in_=dr["sb_mask"][:, j, :])))(), writes=[K.TMPF])
        tr.op("dve", (lambda j=j: (lambda e: e.tensor_copy(out=maskb[:, j, :], in_=K.tmpf[:, 0:384])))(), reads=[K.TMPF], writes=[G0])
    tr.op("dve", lambda e: e.tensor_scalar(out=ntri[:], in0=trif[:], scalar1=-1.0, scalar2=None, op0=ALU.mult), reads=[G0], writes=[G0])
    tr.op("dve", lambda e: e.memset(nones[:], -1.0), writes=[G0])
    QTA = tr.buf("QTA")
    tr.op("pool", lambda e: e.memset(qTa[0][64:128, :], 0.0), writes=[QTA])
    tr.op("pool", lambda e: e.memset(qTa[1][0:64, :], 0.0), writes=[QTA])
    wqkv = dr["sb_w_qkv"]
    for g in range(4):
        norm_in_seq(K, [((lambda blk=g * 8 + b: load_x(K, dr["xseq"], None, blk)), b * 128, K.HT[b]) for b in range(8)], 4)
        for c in range(8):
            wq, WQ = wload(K, wqkv[:, c * 128:(c + 1) * 128], 128)
            wk, WK = wload(K, wqkv[:, D + c * 128:D + (c + 1) * 128], 128)
            wv, WV = wload_extra(K, wqkv[:, 2 * D + c * 128:2 * D + (c + 1) * 128])
            for tt in range(2):
                hs = slice(tt * 512, (tt + 1) * 512)
                pq, pk = K.pab[:, 0, :], K.pab[:, 1, :]
                mm(K, [(pq, [(wq[:, k, 0:128], K.hT[:, k, hs]) for k in range(8)]),
                       (pk, [(wk[:, k, 0:128], K.hT[:, k, hs]) for k in range(8)])],
                   reads=[WQ, WK] + K.HT, writes=K.PAB[0:2])
                tr.op("act", (lambda hs=hs, pq=pq: (lambda e: e.activation(out=qst[:, hs], in_=pq, func=AF.Identity, scale=0.125)))(),
                      reads=[K.PAB[0]], writes=[QST])
                tr.op("dve", (lambda hs=hs, pk=pk: (lambda e: e.tensor_copy(out=kst[:, hs], in_=pk)))(),
                      reads=[K.PAB[1]], writes=[KST])
            gs = slice(g * 1024, (g + 1) * 1024)
            tr.dma("sp", (lambda c=c, gs=gs: (lambda e: e.dma_start(out=qT_d[c, :, gs], in_=qst[:])))(), reads=[QST], writes=[QD[c]])
            tr.dma("sp", (lambda c=c, gs=gs: (lambda e: e.dma_start(out=kT_d[c, :, gs], in_=kst[:])))(), reads=[KST], writes=[KD[c]])
            mm(K, [(K.pab[:, 2 + b // 4, (b % 4) * 128:(b % 4 + 1) * 128],
                    [(K.hT[:, k, b * 128:(b + 1) * 128], wv[:, k, 0:128]) for k in range(8)]) for b in range(8)],
               reads=[WV] + K.HT, writes=K.PAB[2:4])
            for b in range(8):
                tr.op("dve", (lambda b=b, g=g: (lambda e: e.tensor_scalar(
                    out=vst[:, b, :], in0=K.pab[:, 2 + b // 4, (b % 4) * 128:(b % 4 + 1) * 128],
                    scalar1=valid[:, g * 8 + b:g * 8 + b + 1], scalar2=None, op0=ALU.mult)))(),
                    reads=[K.PAB[2 + b // 4], G0], writes=[VST])
            tr.dma("sp", (lambda c=c, gs=gs: (lambda e: e.dma_start(
                out=v_d[c, gs, :].rearrange("(b p) i -> p b i", p=128), in_=vst[:])))(), reads=[VST], writes=[VD[c]])
    tr.barrier()
    oT = K.act[:].rearrange("p a b -> p (a b)").rearrange("p (c t) -> p c t", t=W)
    pz = K.pab[:, 0:2, 0:384]
    prs2 = [K.pab[:, 2:4, 0:384], K.pm2[:, 0:2, 0:384]]
    PRL2 = [K.PAB[2:4], [K.PT, K.PM]]
    po = K.py[:, :].rearrange("p (c t) -> p c t", t=512)[:, :, 0:384]
    PZL, POL = K.PAB[0:2], [K.PY]
    hps = [slice(0, 64), slice(64, 128)]
    units = []
    for qt in range(7):
        kbmax = 13 + 3 * qt
        for kb in range(kbmax, -1, -1):
            units.append((qt, kb, kbmax - kb))
    nU = len(units)

    def S1(c, t):
        qt, kb, j = units[t]
        qs = slice(qt * 384, (qt + 1) * 384)
        sp, SPt = spb[t % 3], SPB[t % 3]
        mm(K, [(pz[:, ch, :], [(kTa[:, kb * 128:(kb + 1) * 128], qTa[ch][:, qs])]) for ch in range(2)],
           reads=[KTA, QTA], writes=PZL)
        tr.op("act", lambda e: e.activation(out=esb[:], in_=pz, func=AF.Exp), reads=PZL, writes=[ESB])
        tr.op("act", lambda e: e.activation(out=sp[:], in_=esb[:], func=AF.Ln, bias=1.0, scale=1.0), reads=[ESB], writes=[SPt])
        if j <= 2:
            for ch in range(2):
                tr.op("dve", (lambda ch=ch: (lambda e: e.tensor_tensor(out=sp[:, ch, :], in0=sp[:, ch, :], in1=maskb[:, 2 - j, :], op=ALU.mult)))(),
                      reads=[SPt, G0], writes=[SPt])
        if kb > 0:
            if j == 0:
                tr.op("dve", lambda e: e.tensor_copy(out=Sf[:], in_=sp[:]), reads=[SPt], writes=[SF])
            else:
                tr.op("dve", lambda e: e.tensor_tensor(out=Sf[:], in0=Sf[:], in1=sp[:], op=ALU.add), reads=[SPt, SF], writes=[SF])
            nb = (t + 1) % 3
            tr.op("dve", lambda e: e.tensor_copy(out=Sb[nb][:], in_=Sf[:]), reads=[SF], writes=[SBB[nb]])

    def S2(c, t):
        qt, kb, j = units[t]
        qs = slice(qt * 384, (qt + 1) * 384)
        sp, SPt = spb[t % 3], SPB[t % 3]
        ab, ABt = abb[t % 2], ABB[t % 2]
        groups = []
        rd = [KTA, QTA, G0, SPt]
        pr, PRL = prs2[t % 2], PRL2[t % 2]
        for ch in range(2):
            pairs = [(kTa[:, kb * 128:(kb + 1) * 128], qTa[ch][:, qs]), (ntri[:], sp[:, ch, :])]
            if j > 0:
                pairs.append((nones[:], Sb[t % 3][:, ch, :]))
            groups.append((pr[:, ch, :], pairs))
        if j > 0:
            rd.append(SBB[t % 3])
        mm(K, groups, reads=rd, writes=PRL)
        tr.op("act", lambda e: e.activation(out=ab[:], in_=pr, func=AF.Exp), reads=PRL, writes=[ABt])
        if j <= 2:
            for ch in range(2):
                tr.op("dve", (lambda ch=ch: (lambda e: e.tensor_tensor(out=ab[:, ch, :], in0=ab[:, ch, :], in1=maskb[:, 2 - j, :], op=ALU.mult)))(),
                      reads=[ABt, G0], writes=[ABt])

    def S3(c, t):
        qt, kb, j = units[t]
        qs = slice(qt * 384, (qt + 1) * 384)
        ab, ABt = abb[t % 2], ABB[t % 2]

        def av(e):
            for ch in range(2):
                ins = e.matmul(po[:, ch, :], lhsT=vca[:, kb, :], rhs=ab[:, ch, :], start=(j == 0), stop=(kb == 0))
            return ins
        tr.op("pe", av, reads=[VCA, ABt], writes=POL)
        if kb == 0:
            for ch in range(2):
                tr.op("dve", (lambda ch=ch: (lambda e: e.tensor_copy(out=oT[hps[ch], c, qs], in_=po[hps[ch], ch, :])))(),
                      reads=POL, writes=[K.ACT[0]])

    for c in range(8):
        tr.dma("sp", (lambda c=c: (lambda e: e.dma_start(out=kTa[:], in_=kT_d[c, :, :])))(), reads=[KD[c]], writes=[KTA])
        for ch in range(2):
            tr.dma("sp", (lambda c=c, ch=ch: (lambda e: e.dma_start(out=qTa[ch][hps[ch], :], in_=qT_d[c, hps[ch], HALO:KS])))(),
                   reads=[QD[c]], writes=[QTA])
        for q4 in range(4):
            tr.dma("sp", (lambda c=c, q4=q4: (lambda e: e.dma_start(
                out=vca[:, q4 * 8:(q4 + 1) * 8, :], in_=v_d[c, q4 * 1024:(q4 + 1) * 1024, :].rearrange("(b p) i -> p b i", p=128))))(),
                reads=[VD[c]], writes=[VCA])
        for t in range(nU + 2):
            if t % 16 == 8 and K.bg:
                K.bg.pop(0)()
            if t < nU:
                S1(c, t)
            if t >= 2:
                S3(c, t - 2)
            if 1 <= t <= nU:
                S2(c, t - 1)
    while K.bg:
        K.bg.pop(0)()
    load_big(K, dr["sb_w_o"], 8)
    for blk in range(NBLK):
        final_proj(K, oT, [K.ACT[0]], 8, blk * 128)
        xt, XB = load_x(K, dr["xseq"], None, blk, row0=HALO + blk * 128)
        norm_out(K, xt, XB, 0, dst_ap, dstB, blk)


def gelu_to(K, src_ps, SRCB, n, out_ap, OUTB):
    tr = K.tr
    i = K.grot
    K.grot = 1 - i
    (t1, T1), (t2, T2) = K.gtmp[i]
    tr.op("act", lambda e: e.activation(out=t1[:, 0:n], in_=src_ps, func=AF.Square, scale=0.044715 ** 0.5),
          reads=[SRCB], writes=[T1])
    tr.op("dve", lambda e: e.scalar_tensor_tensor(out=t1[:, 0:n], in0=t1[:, 0:n], scalar=1.0, in1=src_ps,
                                                   op0=ALU.add, op1=ALU.mult), reads=[T1, SRCB], writes=[T1])
    tr.op("act", lambda e: e.activation(out=t2[:, 0:n], in_=t1[:, 0:n], func=AF.Sigmoid, scale=1.5957691216057308),
          reads=[T1], writes=[T2])
    tr.op("dve", lambda e: e.tensor_tensor(out=out_ap, in0=src_ps, in1=t2[:, 0:n], op=ALU.mult),
          reads=[T2, SRCB], writes=[OUTB])


def mixer1(K, l, src_ap, srcB, dst_ap, dstB):
    tr, dr, sb = K.tr, K.dr, K.larena
    K.lreset()
    if True:
        K.wsTf = K.tmpf[:].rearrange("p (g i) -> p g i", i=128)
        gt = [sb("gtmp%d" % i, [128, 512]) for i in range(2)]
        GT = tr.bufs(2, "GTMP")
        K.gtmp = [((K.sg[0], K.SG[0]), (K.sg[1], K.SG[1])), ((gt[0], GT[0]), (gt[1], GT[1]))]
        K.grot = 0
        K.wo_ar = sb("wo_ar", [128, 10, 1024], BF16)
        K.WOA = tr.buf("WOA")
        K.m01 = sb("m01", [128, 128])
        K.wsT = sb("wsT", [128, 8, 128], BF16)
        K.lngb = sb("lngb", [128, 2048], BF16)
        K.biasB = sb("biasB", [128, 8, 128])
        K.vbf2 = [sb("vbf%d" % i, [128, 2048], BF16) for i in range(2)]
        K.VBF2 = tr.bufs(2, "VBF2_")
        K.vh = sb("vh", [128, 2048], BF16)
        K.bst = sb("bst", [128, 4, 6])
        K.mv = sb("mv", [128, 4])
        K.G1 = tr.buf("G1C")
        K.VBF, K.VH, K.BST, K.MV = tr.buf("VBF"), tr.buf("VH"), tr.buf("BST"), tr.buf("MV")
    G1 = K.G1
    tr.dma("sp", lambda e: e.dma_start(out=K.wsTf, in_=dr["sg_wsT"]), writes=[K.TMPF])
    tr.dma("sp", lambda e: e.dma_start(out=K.m01[:], in_=dr["sg_m01"]), writes=[G1])
    tr.dma("sp", lambda e: e.dma_start(out=K.biasB.rearrange("p g i -> p (g i)"), in_=dr["sg_bias"][0:1, :].to_broadcast([128, 1024])),
           writes=[G1])
    for g in range(8):
        tr.op("dve", (lambda g=g: (lambda e: e.tensor_tensor(out=K.wsT[:, g, :], in0=K.wsTf[:, g, :], in1=K.m01[:], op=ALU.mult)))(),
              reads=[G1, K.TMPF], writes=[G1])
    for q in range(4):
        tr.dma("sp", (lambda q=q: (lambda e: e.dma_start(out=K.row[:], in_=dr["sg_lng"][0:1, q * 512:(q + 1) * 512])))(),
               writes=[K.ROW])
        mm(K, [(K.pm[:, :], [(K.ones_f[0:1, :], K.row[0:1, :])])], reads=[K.CONST, K.ROW], writes=[K.PM])
        tr.op("act", (lambda q=q: (lambda e: e.copy(out=K.lngb[:, q * 512:(q + 1) * 512], in_=K.pm[:, :])))(),
              reads=[K.PM], writes=[G1])
    win = dr["sg_w_in"]
    wv = K.wbig[:, 0:16, :].rearrange("p a b -> p (a b)").rearrange("p (k n) -> p k n", n=2048)
    pv = K.pab[:, :, :]
    psv = K.py[:, :].rearrange("p (c i) -> p c i", i=128)
    for k0 in range(0, 8, 2):
        v = wv[:, k0:k0 + 2, :]
        srcw = win[k0 * 128:(k0 + 2) * 128, 2048:4096].rearrange("(c p) n -> p c n", p=128)
        tr.dma("pool", (lambda v=v, srcw=srcw: (lambda e: e.dma_start(out=v, in_=srcw)))(), writes=[K.WBIG])
    for (k0, k1) in ((0, 3), (3, 6)):
        v = K.wbig[:, 16 + k0:16 + k1, :]
        srcw = dr["sg_w_out"][k0 * 128:k1 * 128, :].rearrange("(c p) n -> p c n", p=128)
        tr.dma("pool", (lambda v=v, srcw=srcw: (lambda e: e.dma_start(out=v, in_=srcw)))(), writes=[K.WBIG])
    for (k0, k1) in ((6, 10), (10, 13), (13, 16)):
        v = K.wo_ar[:, k0 - 6:k1 - 6, :]
        srcw = dr["sg_w_out"][k0 * 128:k1 * 128, :].rearrange("(c p) n -> p c n", p=128)
        tr.dma("pool", (lambda v=v, srcw=srcw: (lambda e: e.dma_start(out=v, in_=srcw)))(), writes=[K.WOA])
    for s in range(NST):
        norm_in_seq(K, [((lambda blk=s * SB + b: load_x(K, src_ap, srcB, blk)), b * 128, K.HT[b]) for b in range(SB)], 4)
        for pc in range(4):
            wt, WB = wload(K, win[:, pc * 512:(pc + 1) * 512], 512)
            for fc in range(4):
                f = pc * 4 + fc
                for tt in range(2):
                    bank = (fc * 2 + tt) % 4
                    pg = K.pab[:, bank, 0:TT]
                    hs = slice(tt * TT, (tt + 1) * TT)
                    mm(K, [(pg, [(wt[:, k, fc * 128:(fc + 1) * 128], K.hT[:, k, hs]) for k in range(8)])],
                       reads=[WB] + K.HT, writes=[K.PAB[bank]])
                    gelu_to(K, pg, K.PAB[bank], TT, K.act[:, f, hs], K.ACT[f])
        def stageA(b):
            bs = slice(b * 128, (b + 1) * 128)
            vb, VB_ = K.vbf2[b % 2], K.VBF2[b % 2]
            for n in range(4):
                mm(K, [(pv[:, n, :], [(K.hT[:, k, bs], wv[:, k, n * 512:(n + 1) * 512]) for k in range(8)])],
                   reads=[K.WBIG, K.HT[b]], writes=[K.PAB[n]])
            for n in range(4):
                gelu_to(K, pv[:, n, :], K.PAB[n], 512, vb[:, n * 512:(n + 1) * 512], VB_)

        def stageB(b):
            bs = slice(b * 128, (b + 1) * 128)
            vb, VB_ = K.vbf2[b % 2], K.VBF2[b % 2]
            for n in range(4):
                tr.op("dve", (lambda n=n: (lambda e: e.bn_stats(out=K.bst[:, n, :], in_=vb[:, n * 512:(n + 1) * 512])))(),
                      reads=[VB_], writes=[K.BST])
            tr.op("dve", lambda e: e.bn_aggr(out=K.mv[:, 0:2], in_=K.bst[:].rearrange("p a b -> p (a b)")), reads=[K.BST], writes=[K.MV])
            tr.op("act", lambda e: e.activation(out=K.mv[:, 2:3], in_=K.mv[:, 1:2], func=AF.Sqrt, bias=K.epsT[:, 0:1], scale=1.0),
                  reads=[K.MV, K.CONST], writes=[K.MV])
            tr.op("dve", lambda e: e.reciprocal(out=K.mv[:, 2:3], in_=K.mv[:, 2:3]), reads=[K.MV], writes=[K.MV])
            tr.op("dve", lambda e: e.tensor_scalar(out=K.mv[:, 3:4], in0=K.mv[:, 0:1], scalar1=K.mv[:, 2:3], scalar2=-1.0,
                                                    op0=ALU.mult, op1=ALU.mult), reads=[K.MV], writes=[K.MV])
            for hf in range(2):
                cs = slice(hf * 1024, (hf + 1) * 1024)
                tr.op("act", (lambda cs=cs: (lambda e: e.activation(out=K.tmpf[:], in_=vb[:, cs], func=AF.Identity,
                                                                    scale=K.mv[:, 2:3], bias=K.mv[:, 3:4])))(),
                      reads=[VB_, K.MV], writes=[K.TMPF])
                tr.op("dve", (lambda cs=cs: (lambda e: e.tensor_tensor(out=K.vh[:, cs], in0=K.tmpf[:], in1=K.lngb[:, cs], op=ALU.mult)))(),
                      reads=[K.TMPF, G1], writes=[K.VH])
            for hf in range(2):
                groups = []
                for cc in range(8):
                    ch = hf * 8 + cc
                    g = ch // 2
                    groups.append((psv[:, cc, :], [(K.vh[:, ch * 128:(ch + 1) * 128], K.wsT[:, g, :])]))
                mm(K, groups, reads=[K.VH, G1], writes=[K.PY])
                tmp4 = K.tmpf[:].rearrange("p (c i) -> p c i", i=128)
                for par in range(2):
                    tr.op("dve", (lambda hf=hf, par=par: (lambda e: e.tensor_tensor(
                        out=tmp4[:, par:8:2, :], in0=psv[:, par:8:2, :], in1=K.biasB[:, 4 * hf:4 * hf + 4, :], op=ALU.add)))(),
                        reads=[K.PY, G1], writes=[K.TMPF])
                tr.op("dve", (lambda hf=hf, bs=bs: (lambda e: e.tensor_tensor(
                    out=K.act[:, hf * 8:(hf + 1) * 8, bs], in0=tmp4, in1=K.act[:, hf * 8:(hf + 1) * 8, bs], op=ALU.mult)))(),
                    reads=[K.TMPF] + K.ACT[hf * 8:(hf + 1) * 8], writes=K.ACT[hf * 8:(hf + 1) * 8])

        for b in range(SB + 1):
            if b < SB:
                stageA(b)
            if b >= 1:
                stageB(b - 1)
        for b in range(SB):
            blk = s * SB + b
            final_proj(K, K.act, K.ACT[0:16], 16, b * 128, reads_extra=[K.WOA],
                       wsel=lambda k: (K.wbig[:, 16 + k, :] if k < 6 else K.wo_ar[:, k - 6, :]))
            xt, XB = load_x(K, src_ap, srcB, blk)
            norm_out(K, xt, XB, 0, dst_ap, dstB, blk)


def mixer3(K, l, src_ap, srcB, dst_ap, dstB):
    tr, dr, sb = K.tr, K.dr, K.larena
    K.lreset()
    kcar = sb("kcar", [128, 8, 512], BF16)
    vcar = sb("vcar", [128, 8, 4, 128], BF16)
    qT2 = sb("qT2", [128, 2, ST], BF16)
    kTc = sb("kTc", [128, 1408], BF16)
    vc = sb("vc", [128, 11, 128], BF16)
    pTb = [sb("pTb%d" % i, [128, 5, 2, 128], BF16) for i in range(2)]
    Et = sb("Et", [128, 16, 2, 128], BF16)
    pexp = [sb("pexp%d" % i, [128, 2, 2, 128]) for i in range(2)]
    bfar = sb("bfar", [128, 16])
    m04 = sb("m04", [128, 2, 128])
    rden = sb("rden", [128, 128])
    G3 = tr.buf("G3")
    KCAR, VCAR, QTC, KTC, VC, RDEN = (tr.buf(n) for n in ("KCAR", "VCAR", "QTC", "KTC", "VC", "RDEN"))
    PEXP, PTB = tr.bufs(2, "PEXP"), tr.bufs(2, "PTB")
    hps = [slice(0, 64), slice(64, 128)]
    tr.dma("sp", lambda e: e.dma_start(out=bfar[:], in_=dr["cb_bfar"]), writes=[G3])
    tr.dma("sp", lambda e: e.dma_start(out=m04[:], in_=dr["cb_m04"]), writes=[G3])
    for ch in range(2):
        o = 1 - ch
        tr.op("pool", (lambda ch=ch, o=o: (lambda e: e.memset(qT2[hps[o], ch, :], 0.0)))(), writes=[QTC])
    etf = K.tmpf[:].rearrange("p (h r i) -> p h r i", h=4, r=2)
    for q in range(4):
        tr.dma("sp", (lambda q=q: (lambda e: e.dma_start(out=etf, in_=dr["cb_bt"][:, 4 * q:4 * q + 4, :, :])))(), writes=[K.TMPF])
        tr.op("act", lambda e: e.activation(out=K.tmpf[:], in_=K.tmpf[:], func=AF.Exp), reads=[K.TMPF], writes=[K.TMPF])
        for hq in range(4):
            h = 4 * q + hq
            tr.op("dve", (lambda h=h, hq=hq: (lambda e: e.tensor_copy(out=Et[:, h, 0, :], in_=etf[:, hq, 0, :])))(),
                  reads=[K.TMPF], writes=[G3])
            tr.op("dve", (lambda h=h, hq=hq: (lambda e: e.tensor_tensor(out=Et[:, h, 1, :], in0=etf[:, hq, 1, :], in1=m04[:, 1, :], op=ALU.mult)))(),
                  reads=[K.TMPF, G3], writes=[G3])
    load_big(K, dr["cb_w_o"], 8)
    wqkv = dr["cb_w_qkv"]
    pz = K.pab[:].rearrange("p a b -> p (a b)")[:, 0:1280].rearrange("p (r ch i) -> p r ch i", ch=2, i=128)
    po, pden = K.pm[:, 0:256], K.pm[:, 256:512]
    po3 = po.rearrange("p (ch i) -> p ch i", i=128)
    pden3 = pden.rearrange("p (ch i) -> p ch i", i=128)

    def S1(s, c, b):
        bs = slice(b * 128, (b + 1) * 128)
        r0 = max(0, 4 - (s * SB + b))
        sl = b % 2
        pt_, PT_ = pTb[sl], PTB[sl]
        pe_, PE_ = pexp[sl], PEXP[sl]
        mm(K, [(pz[:, r, :, :], [(kTc[:, (b + r) * 128:(b + r + 1) * 128], qT2[:, :, bs])]) for r in range(r0, 5)],
           reads=[KTC, QTC], writes=K.PAB)
        if r0 <= 2:
            for ch in range(2):
                h = 2 * c + ch
                tr.op("act", (lambda ch=ch, h=h: (lambda e: e.activation(out=pt_[:, r0:3, ch, :], in_=pz[:, r0:3, ch, :], func=AF.Exp,
                                                                          bias=bfar[:, h:h + 1], scale=1.0)))(),
                      reads=K.PAB + [G3], writes=[PT_])
        rlo = max(r0, 3)
        tr.op("act", lambda e: e.activation(out=pe_[:, rlo - 3:2, :, :], in_=pz[:, rlo:5, :, :], func=AF.Exp), reads=K.PAB, writes=[PE_])
        if r0 == 0:
            for ch in range(2):
                tr.op("dve", (lambda ch=ch: (lambda e: e.tensor_tensor(out=pt_[:, 0, ch, :], in0=pt_[:, 0, ch, :], in1=m04[:, 0, :], op=ALU.mult)))(),
                      reads=[PT_, G3], writes=[PT_])
        if r0 <= 3:
            tr.op("dve", lambda e: e.tensor_tensor(out=pt_[:, 3, :, :], in0=pe_[:, 0, :, :], in1=Et[:, 2 * c:2 * c + 2, 0, :], op=ALU.mult),
                  reads=[PE_, G3], writes=[PT_])
        tr.op("dve", lambda e: e.tensor_tensor(out=pt_[:, 4, :, :], in0=pe_[:, 1, :, :], in1=Et[:, 2 * c:2 * c + 2, 1, :], op=ALU.mult),
              reads=[PE_, G3], writes=[PT_])

    def S2(s, c, b):
        bs = slice(b * 128, (b + 1) * 128)
        r0 = max(0, 4 - (s * SB + b))
        sl = b % 2
        pt_, PT_ = pTb[sl], PTB[sl]
        rs = list(range(r0, 5))
        mm(K, [(po3, [(vc[:, b + r, :], pt_[:, r, :, :]) for r in rs]),
               (pden3, [(K.ones_b[:, :], pt_[:, r, :, :]) for r in rs])],
           reads=[VC, PT_, K.CONST], writes=[K.PM])
        for ch in range(2):
            tr.op("dve", (lambda ch=ch: (lambda e: e.reciprocal(out=rden[hps[ch], :], in_=pden3[hps[ch], ch, :])))(),
                  reads=[K.PM], writes=[RDEN])
        for ch in range(2):
            tr.op("dve", (lambda ch=ch: (lambda e: e.tensor_tensor(out=K.act[hps[ch], c, bs], in0=po3[hps[ch], ch, :], in1=rden[hps[ch], :], op=ALU.mult)))(),
                  reads=[K.PM, RDEN], writes=[K.ACT[c]])

    for s in range(NST):
        norm_in_seq(K, [((lambda blk=s * SB + b: load_x(K, src_ap, srcB, blk)), b * 128, K.HT[b]) for b in range(SB)], 4)
        for c in range(8):
            wq, WQ = wload(K, wqkv[:, c * 128:(c + 1) * 128], 128)
            wk, WK = wload(K, wqkv[:, D + c * 128:D + (c + 1) * 128], 128)
            wv, WV = wload(K, wqkv[:, 2 * D + c * 128:2 * D + (c + 1) * 128], 128)
            if s > 0:
                tr.op("pool", (lambda c=c: (lambda e: e.tensor_copy(out=kTc[:, 0:512], in_=kcar[:, c, :])))(),
                      reads=[KCAR], writes=[KTC])
                tr.op("pool", (lambda c=c: (lambda e: e.tensor_copy(out=vc[:, 0:4, :], in_=vcar[:, c, :, :])))(),
                      reads=[VCAR], writes=[VC])
            for tt in range(2):
                hs = slice(tt * TT, (tt + 1) * TT)
                pq, pk = K.pab[:, 0, 0:TT], K.pab[:, 1, 0:TT]
                mm(K, [(pq, [(wq[:, k, 0:128], K.hT[:, k, hs]) for k in range(8)]),
                       (pk, [(wk[:, k, 0:128], K.hT[:, k, hs]) for k in range(8)])],
                   reads=[WQ, WK] + K.HT[0:SB], writes=K.PAB[0:2])
                for ch in range(2):
                    tr.op("act", (lambda hs=hs, ch=ch: (lambda e: e.activation(out=qT2[hps[ch], ch, hs], in_=K.pab[hps[ch], 0, 0:TT],
                                                                               func=AF.Identity, scale=0.125)))(),
                          reads=[K.PAB[0]], writes=[QTC])
                tr.op("dve", (lambda tt=tt, pk=pk: (lambda e: e.tensor_copy(out=kTc[:, 512 + tt * TT:512 + (tt + 1) * TT], in_=pk)))(),
                      reads=[K.PAB[1]], writes=[KTC])
            mm(K, [(K.pab[:, 2 + b // 4, (b % 4) * 128:(b % 4 + 1) * 128],
                    [(K.hT[:, k, b * 128:(b + 1) * 128], wv[:, k, 0:128]) for k in range(8)]) for b in range(SB)],
               reads=[WV] + K.HT[0:SB], writes=K.PAB[2:4])
            tr.op("dve", lambda e: e.tensor_copy(out=vc[:, 4:8, :], in_=K.pab[:, 2, :].rearrange("p (b i) -> p b i", i=128)),
                  reads=[K.PAB[2]], writes=[VC])
            tr.op("dve", lambda e: e.tensor_copy(out=vc[:, 8:11, :], in_=K.pab[:, 3, 0:384].rearrange("p (b i) -> p b i", i=128)),
                  reads=[K.PAB[3]], writes=[VC])
            tr.op("pool", (lambda c=c: (lambda e: e.tensor_copy(out=kcar[:, c, :], in_=kTc[:, 896:1408])))(),
                  reads=[KTC], writes=[KCAR])
            tr.op("pool", (lambda c=c: (lambda e: e.tensor_copy(out=vcar[:, c, :, :], in_=vc[:, 7:11, :])))(),
                  reads=[VC], writes=[VCAR])
            for b in range(SB + 1):
                if b < SB:
                    S1(s, c, b)
                if b >= 1:
                    S2(s, c, b - 1)
        for b in range(SB):
            blk = s * SB + b
            final_proj(K, K.act, K.ACT[0:8], 8, b * 128)
            xt, XB = load_x(K, src_ap, srcB, blk)
            norm_out(K, xt, XB, 0, dst_ap, dstB, blk)


def _consts():
    c = {}
    c["ident"] = np.eye(128, dtype=np.float32)
    p = np.arange(128)[:, None]
    f = np.arange(384)[None, :]
    c["sb_mask"] = np.stack([(f > p + 128 * j) for j in range(3)], axis=1).astype(np.float32)
    j = np.arange(128)[:, None]
    s = np.arange(128)[None, :]
    c["sb_tri"] = (j >= s).astype(np.float32)
    ch = np.arange(128) // 64
    c["sg_m01"] = (ch[None, :] >= ch[:, None]).astype(np.float32)
    pp = np.arange(128)[:, None]
    ff = np.arange(128)[None, :]
    m0 = ~((pp < 64) & (ff >= 64))
    m4 = ~((pp >= 64) & (ff < 64))
    c["cb_m04"] = np.stack([m0, m4], axis=1).astype(np.float32)
    return c


def _core_inputs(layers, inp, b, half, xfull):
    base = 0 if half == 0 else HALO
    cst = _consts()
    m = {}
    if 0 in layers:
        xseq = np.zeros((KS, D), np.float32)
        if half == 0:
            xseq[HALO:] = xfull[:W]
        else:
            xseq[:] = xfull
        m["xseq"] = xseq
        tok = np.arange(KS).reshape(32, 128).T
        m["valid"] = ((tok + base - HALO) >= 0).astype(np.float32)
        m["sb_w_qkv"] = inp["sb_w_qkv"][0]
        m["sb_w_o"] = inp["sb_w_o"][0]
        m["sb_mask"] = cst["sb_mask"]
        m["sb_tri"] = cst["sb_tri"]
    else:
        m["xin"] = np.ascontiguousarray(xfull[base:base + W])
    m["cfm"] = np.ascontiguousarray(inp["c"][b].reshape(8, 128).T)
    m["ident"] = cst["ident"]
    m["ngfm"] = np.ascontiguousarray(inp["norm_g"].reshape(16, 8, 128).transpose(2, 0, 1).reshape(128, 128))
    m["ngrow"] = np.ascontiguousarray(inp["norm_g"].reshape(1, 16 * D))
    m["ada_w"] = inp["ada_w"][list(layers)]
    m["ada_b"] = np.ascontiguousarray(inp["ada_b"].reshape(1, -1))
    m["ffn_w_in"] = inp["ffn_w_in"][list(layers)]
    m["ffn_w_out"] = inp["ffn_w_out"][list(layers)]
    if 1 in layers:
        m["sg_w_in"] = inp["sg_w_in"][0]
        m["sg_lng"] = np.ascontiguousarray(inp["sg_ln_g"][0].reshape(1, 2048))
        m["sg_wsT"] = np.ascontiguousarray(inp["sg_w_s"][0].transpose(2, 0, 1))
        m["sg_m01"] = cst["sg_m01"]
        m["sg_bias"] = np.ascontiguousarray(inp["sg_bias"][0].reshape(1, 1024))
        m["sg_w_out"] = inp["sg_w_out"][0]
    if 2 in layers:
        m["sc_w_in"] = inp["sc_w_in"][0]
        m["sc_cw"] = np.ascontiguousarray(inp["sc_conv_w"][0].reshape(3, 8, 128).transpose(2, 0, 1).reshape(128, 24))
        m["sc_w_out"] = inp["sc_w_out"][0]
    if 3 in layers:
        m["cb_w_qkv"] = inp["cb_w_qkv"][0]
        m["cb_w_o"] = inp["cb_w_o"][0]
        rb = inp["cb_rel_bias"][0]
        p = np.arange(128)[:, None]
        f = np.arange(128)[None, :]
        idx3 = np.clip(f - p + 128, -128, 128) + 128
        idx4 = np.clip(f - p, -128, 128) + 128
        bt = np.stack([rb[:, idx3], rb[:, idx4]], axis=1)
        m["cb_bt"] = np.ascontiguousarray(bt.transpose(2, 0, 1, 3))
        m["cb_bfar"] = np.ascontiguousarray(np.broadcast_to(rb[:, 256][None, :], (128, 16)))
        m["cb_m04"] = cst["cb_m04"]
    return {k: np.ascontiguousarray(v, dtype=np.float32) for k, v in m.items()}


LAUNCH_PLAN = [[0, 1, 2, 3]]
_NC_CACHE = {}


def run_layers(layers, inp, x, batches=(0, 1, 2, 3)):
    key = tuple(layers)
    if key not in _NC_CACHE:
        _NC_CACHE[key] = build(list(layers))
    nc = _NC_CACHE[key]
    in_maps = []
    for b in batches:
        for half in (0, 1):
            in_maps.append(_core_inputs(layers, inp, b, half, x[b]))
    res = run_bass_kernel_spmd(nc, in_maps, core_ids=list(range(len(in_maps))))
    out = np.empty_like(x)
    for i, b in enumerate(batches):
        a = res.results[2 * i]["xout"]
        bb = res.results[2 * i + 1]["xout"]
        out[b, :2048] = a[:2048]
        out[b, 2048:] = bb[2048 - HALO:]
    return out, res


def kernel(**inputs):
    inp = {k: np.asarray(v, dtype=np.float32) for k, v in inputs.items()}
    x = inp["x"]
    for layers in LAUNCH_PLAN:
        x, _ = run_layers(layers, inp, x)
    return x.astype(np.float32)
```

```python
import numpy as np
from contextlib import ExitStack
import concourse.bass as bass
import concourse.mybir as mybir
from concourse.bass_utils import run_bass_kernel_spmd

F32 = mybir.dt.float32
BF16 = mybir.dt.bfloat16
AF = mybir.ActivationFunctionType
ALU = mybir.AluOpType

D = 1024
W = 2688
NBLK = 21
NST = 3
SB = 7
ST = 896
TT = 448
KS = 4096
HALO = 1408
DFF = 2816
N_DMA_SLOTS = 24
EPS = 1e-6


class Buf:
    __slots__ = ("name", "last_w", "readers")

    def __init__(self, name):
        self.name = name
        self.last_w = None
        self.readers = {}


class Tracker:
    ENG = ("pe", "act", "dve", "pool", "sp")

    def __init__(self):
        self.prog = {e: [] for e in self.ENG}
        self.count = {e: 0 for e in self.ENG}
        self.unit = {e: 1 for e in self.ENG}
        for k in range(N_DMA_SLOTS):
            self.count["dma%d" % k] = 0
            self.unit["dma%d" % k] = 16
        self.waited = {e: {} for e in self.ENG}
        self.next_slot = {"sp": 0, "pool": 0}
        self.nbuf = 0

    def buf(self, name=None):
        self.nbuf += 1
        return Buf(name or ("b%d" % self.nbuf))

    def bufs(self, n, name="b"):
        return [self.buf("%s%d" % (name, i)) for i in range(n)]

    def _need(self, eng, dep):
        if dep is None:
            return
        p, idx = dep
        if p == eng and eng == "pe":
            return
        if self.waited[eng].get(p, 0) >= idx:
            return
        self.waited[eng][p] = idx
        self.prog[eng].append(("wait", p, idx * self.unit[p]))

    def _deps(self, eng, reads, writes):
        for b in reads:
            self._need(eng, b.last_w)
        for b in writes:
            self._need(eng, b.last_w)
            for p, idx in list(b.readers.items()):
                self._need(eng, (p, idx))

    @staticmethod
    def _flat(bs):
        out = []
        for b in bs:
            if isinstance(b, (list, tuple)):
                out.extend(Tracker._flat(b))
            else:
                out.append(b)
        return out

    def op(self, eng, fn, reads=(), writes=()):
        reads, writes = self._flat(reads), self._flat(writes)
        self._deps(eng, reads, writes)
        self.count[eng] += 1
        idx = self.count[eng]
        self.prog[eng].append(("op", fn, eng))
        for b in reads:
            b.readers[eng] = idx
        for b in writes:
            b.last_w = (eng, idx)
            b.readers = {}
        return idx

    def dma(self, queue, fn, reads=(), writes=()):
        reads, writes = self._flat(reads), self._flat(writes)
        half = N_DMA_SLOTS // 2
        base = 0 if queue == "sp" else half
        k = base + self.next_slot[queue]
        self.next_slot[queue] = (self.next_slot[queue] + 1) % half
        slot = "dma%d" % k
        if self.count[slot] > 0:
            self._need(queue, (slot, self.count[slot]))
        self._deps(queue, reads, writes)
        self.count[slot] += 1
        idx = self.count[slot]
        self.prog[queue].append(("dma", fn, slot))
        for b in reads:
            b.readers[slot] = idx
        for b in writes:
            b.last_w = (slot, idx)
            b.readers = {}

    def barrier(self):
        prods = list(self.count.keys())
        for eng in self.ENG:
            for p in prods:
                if self.count[p] > 0:
                    self._need(eng, (p, self.count[p]))

    def final_wait(self, eng, bufs):
        for b in bufs:
            self._need(eng, b.last_w)

    def emit(self, nc, sems):
        def run(engname, engine):
            for item in self.prog[engname]:
                if item[0] == "wait":
                    engine.wait_ge(sems[item[1]], item[2])
                elif item[0] == "op":
                    item[1](engine).then_inc(sems[item[2]], 1)
                else:
                    item[1](engine).then_inc(sems[item[2]], 16)
        with nc.Block() as block:
            @block.sync
            def _(e):
                run("sp", e)

            @block.tensor
            def _(e):
                run("pe", e)

            @block.scalar
            def _(e):
                run("act", e)

            @block.vector
            def _(e):
                run("dve", e)

            @block.gpsimd
            def _(e):
                run("pool", e)


class Ctx:
    pass


def build(layers):
    nc = bass.Bass("TRN2", target_bir_lowering=False)
    tr = Tracker()
    K = Ctx()
    K.nc, K.tr = nc, tr
    has0 = 0 in layers

    def din(name, shape):
        return nc.dram_tensor(name, list(shape), F32, kind="ExternalInput").ap()

    dr = {}
    if has0:
        dr["xseq"] = din("xseq", [KS, D])
        dr["valid"] = din("valid", [128, 32])
        dr["sb_w_qkv"] = din("sb_w_qkv", [D, 3 * D])
        dr["sb_w_o"] = din("sb_w_o", [D, D])
        dr["sb_mask"] = din("sb_mask", [128, 3, 384])
        dr["sb_tri"] = din("sb_tri", [128, 128])
    else:
        dr["xin"] = din("xin", [W, D])
    dr["cfm"] = din("cfm", [128, 8])
    dr["ident"] = din("ident", [128, 128])
    dr["ngfm"] = din("ngfm", [128, 128])
    dr["ngrow"] = din("ngrow", [1, 16 * D])
    dr["ada_w"] = din("ada_w", [len(layers), D, 6 * D])
    dr["ada_b"] = din("ada_b", [1, 4 * 6 * D])
    dr["ffn_w_in"] = din("ffn_w_in", [len(layers), D, 2 * DFF])
    dr["ffn_w_out"] = din("ffn_w_out", [len(layers), DFF, D])
    if 1 in layers:
        dr["sg_w_in"] = din("sg_w_in", [D, 4096])
        dr["sg_lng"] = din("sg_lng", [1, 2048])
        dr["sg_wsT"] = din("sg_wsT", [128, 8, 128])
        dr["sg_m01"] = din("sg_m01", [128, 128])
        dr["sg_bias"] = din("sg_bias", [1, 1024])
        dr["sg_w_out"] = din("sg_w_out", [2048, D])
    if 2 in layers:
        dr["sc_w_in"] = din("sc_w_in", [D, 3 * D])
        dr["sc_cw"] = din("sc_cw", [128, 24])
        dr["sc_w_out"] = din("sc_w_out", [D, D])
    if 3 in layers:
        dr["cb_w_qkv"] = din("cb_w_qkv", [D, 3 * D])
        dr["cb_w_o"] = din("cb_w_o", [D, D])
        dr["cb_bt"] = din("cb_bt", [128, 16, 2, 128])
        dr["cb_bfar"] = din("cb_bfar", [128, 16])
        dr["cb_m04"] = din("cb_m04", [128, 2, 128])
    xout = nc.dram_tensor("xout", [W, D], F32, kind="ExternalOutput").ap()
    xs = nc.dram_tensor("xs", [W, D], F32, kind="Internal").ap()
    K.dr = dr
    K.lpos = {l: i for i, l in enumerate(layers)}

    with ExitStack() as es:
        def sb(name, shape, dt=F32):
            return es.enter_context(nc.sbuf_tensor("sb_" + name, list(shape), dt))

        def ps(name, shape, dt=F32):
            return es.enter_context(nc.psum_tensor("ps_" + name, list(shape), dt))
        K.sb, K.ps = sb, ps
        sems = {}
        for e in list(Tracker.ENG) + ["dma%d" % k for k in range(N_DMA_SLOTS)]:
            sems[e] = es.enter_context(nc.semaphore("sem_" + e))

        K.py = ps("py", [128, 1024])
        K.pab = ps("pab", [128, 4, 512])
        K.pm2 = ps("pm2", [128, 2, 512])
        K.pT = K.pm2[:, 0, :].bitcast(BF16).rearrange("p (c i) -> p c i", i=128)
        K.pm = K.pm2[:, 1, :]
        K.PY = tr.buf("PY")
        K.PAB = tr.bufs(4, "PAB")
        K.PT = tr.buf("PT")
        K.PM = tr.buf("PM")

        K.idf = sb("idf", [128, 128])
        K.idb = sb("idb", [128, 128], BF16)
        K.ones_f = sb("ones_f", [128, 128])
        K.ones_b = sb("ones_b", [128, 128], BF16)
        K.epsT = sb("epsT", [128, 1])
        K.ngfm = sb("ngfm", [128, 128])
        K.cfm = sb("cfm", [128, 8])
        K.csil = sb("csil", [128, 8], BF16)
        K.CONST = tr.buf("CONST")
        C = K.CONST
        tr.dma("sp", lambda e: e.dma_start(out=K.idf[:], in_=dr["ident"]), writes=[C])
        tr.dma("sp", lambda e: e.dma_start(out=K.ngfm[:], in_=dr["ngfm"]), writes=[C])
        tr.dma("sp", lambda e: e.dma_start(out=K.cfm[:], in_=dr["cfm"]), writes=[C])
        tr.op("dve", lambda e: e.tensor_copy(out=K.idb[:], in_=K.idf[:]), reads=[C], writes=[C])
        tr.op("dve", lambda e: e.memset(K.ones_f[:], 1.0), writes=[C])
        tr.op("dve", lambda e: e.memset(K.ones_b[:], 1.0), writes=[C])
        tr.op("dve", lambda e: e.memset(K.epsT[:], EPS), writes=[C])
        tr.op("act", lambda e: e.activation(out=K.csil[:], in_=K.cfm[:], func=AF.Silu), reads=[C], writes=[C])

        K.xb = [sb("xb%d" % i, [128, D]) for i in range(2)]
        K.XB = tr.bufs(2, "XB")
        K.xrot = 0
        K.junk = sb("junk", [128, D], BF16)
        K.JUNK = tr.buf("JUNK")
        K.xn2 = [sb("xn%d" % i, [128, D], BF16) for i in range(2)]
        K.XN2 = tr.bufs(2, "XN")
        K.xnrot = 0
        K.st4 = [sb("st4_%d" % i, [128, 4]) for i in range(2)]
        K.ST4 = tr.bufs(2, "ST4")
        K.strot = 0
        K.hT = sb("hT", [128, 8, 1024], BF16)
        K.HT = tr.bufs(8, "HT")
        K.tmpf = sb("tmpf", [128, D])
        K.TMPF = tr.buf("TMPF")
        K.modc_all = sb("modc", [128, 4, 6, 8])
        K.MODCL = tr.bufs(4, "MODC")
        K.gsc = nc.dram_tensor("gsc", [4, 2, D], F32, kind="Internal").ap()
        K.GSC = [tr.bufs(2, "GSC%d_" % l_) for l_ in range(4)]
        K.bg = []
        K.gg = [sb("gg%d" % i, [128, D]) for i in range(2)]
        K.GG = tr.bufs(2, "GG")
        K.row = sb("row", [1, 512])
        K.ROW = tr.buf("ROW")
        K.row2 = sb("row2", [1, 512])
        K.ROW2 = tr.buf("ROW2")
        K.wstr = sb("wstr", [128, 8, 1024], BF16)
        K.WSTR = tr.bufs(8, "WSTR")
        K.wrot = {128: 0, 256: 0, 512: 0}
        K.wbig = sb("wbig", [128, 22, D], BF16)
        K.WBIG = tr.buf("WBIG")
        K.act = sb("actT", [128, 24, ST], BF16)
        K.ACT = tr.bufs(24, "ACT")
        K.sg = [sb("sgt%d" % i, [128, 512]) for i in range(2)]
        K.arena = {F32: sb("arena_f", [128, 2304]), BF16: sb("arena_b", [128, 21120], BF16)}
        K.aoff = {F32: 0, BF16: 0}

        def larena(name, shape, dt=F32):
            n = 1
            for d_ in shape[1:]:
                n *= d_
            off = K.aoff[dt]
            K.aoff[dt] = off + n
            t = K.arena[dt]
            assert off + n <= (2304 if dt == F32 else 21120), (name, off, n)
            ap = t[0:shape[0], off:off + n]
            if len(shape) == 3:
                ap = ap.rearrange("p (a b) -> p a b", a=shape[1], b=shape[2])
            elif len(shape) == 4:
                ap = ap.rearrange("p (a b c) -> p a b c", a=shape[1], b=shape[2], c=shape[3])
            return ap

        def lreset():
            tr.barrier()
            K.aoff = {F32: 0, BF16: 0}
        K.larena, K.lreset = larena, lreset
        K.SG = tr.bufs(2, "SG")
        K.sgrot = 0
        K.XS = tr.bufs(NBLK, "XS")
        K.XO = tr.bufs(NBLK, "XO")

        src = None
        n_sub = 2 * len(layers)
        sub = 0
        for t_ in adaln_tasks(K, layers[0]):
            t_()
        for l in layers[1:]:
            K.bg.extend(adaln_tasks(K, l))
        if not has0:
            while K.bg:
                K.bg.pop(0)()
        for l in layers:
            adaln_fetch(K, l)
            dst = xout if sub == n_sub - 1 else xs
            dstB = K.XO if sub == n_sub - 1 else K.XS
            if l == 0:
                mixer0(K, l, dst, dstB)
            else:
                s_ap, s_B = (dr["xin"], None) if sub == 0 else (xs, K.XS)
                [None, mixer1, mixer2, mixer3][l](K, l, s_ap, s_B, dst, dstB)
            sub += 1
            dst = xout if sub == n_sub - 1 else xs
            dstB = K.XO if sub == n_sub - 1 else K.XS
            ffn(K, l, xs, K.XS, dst, dstB)
            sub += 1
        tr.final_wait("sp", K.XO)
        tr.emit(nc, sems)
    return nc


def mm(K, groups, reads, writes):
    def fn(e):
        ins = None
        for out_ap, pairs in groups:
            n = len(pairs)
            for i, (l, r) in enumerate(pairs):
                ins = e.matmul(out_ap, lhsT=l, rhs=r, start=(i == 0), stop=(i == n - 1))
        return ins
    K.tr.op("pe", fn, reads=reads, writes=writes)


def wload(K, dram_ap, ncols, kchunks=8):
    nslots = 1024 // ncols
    i = K.wrot[ncols]
    K.wrot[ncols] = (i + 1) % nslots
    view = K.wstr[:, 0:kchunks, i * ncols:(i + 1) * ncols]
    u = ncols // 128
    bufs = K.WSTR[i * u:(i + 1) * u]
    K.tr.dma("pool", lambda e: e.dma_start(out=view, in_=dram_ap.rearrange("(c p) n -> p c n", p=128)),
             writes=bufs)
    return view, bufs


def load_big(K, dram_ap, kchunks):
    step = 4
    for k0 in range(0, kchunks, step):
        k1 = min(kchunks, k0 + step)
        v = K.wbig[:, k0:k1, :]
        src = dram_ap[k0 * 128:k1 * 128, :].rearrange("(c p) n -> p c n", p=128)
        K.tr.dma("pool", (lambda v=v, src=src: (lambda e: e.dma_start(out=v, in_=src)))(), writes=[K.WBIG])


def adaln_tasks(K, l):
    tr, dr = K.tr, K.dr
    g_m = (l * 4 + 1) * D
    g_f = (l * 4 + 3) * D
    MC = K.MODCL[l]
    tasks = []

    def block(j):
        vec, half = j // 2, j % 2
        wt, WB = wload(K, dr["ada_w"][K.lpos[l], :, j * 512:(j + 1) * 512], 512)
        mm(K, [(K.pm[0:1, :], [(K.csil[:, k:k + 1], wt[:, k, :]) for k in range(8)])],
           reads=[K.CONST, WB], writes=[K.PM])
        off = l * 6 * D + j * 512
        tr.dma("sp", lambda e: e.dma_start(out=K.row[:], in_=dr["ada_b"][0:1, off:off + 512]), writes=[K.ROW])
        tr.op("dve", lambda e: e.tensor_tensor(out=K.row[:], in0=K.pm[0:1, :], in1=K.row[:], op=ALU.add),
              reads=[K.PM, K.ROW], writes=[K.ROW])
        if vec in (2, 5):
            goff = (g_m if vec == 2 else g_f) + half * 512
            gi = 0 if vec == 2 else 1
            tr.dma("sp", lambda e: e.dma_start(out=K.row2[0:1, 0:512], in_=dr["ngrow"][0:1, goff:goff + 512]), writes=[K.ROW2])
            tr.op("dve", lambda e: e.tensor_tensor(out=K.row[:], in0=K.row[:], in1=K.row2[0:1, 0:512], op=ALU.mult),
                  reads=[K.ROW, K.ROW2], writes=[K.ROW])
            tr.dma("sp", lambda e: e.dma_start(out=K.gsc[l, gi:gi + 1, half * 512:(half + 1) * 512], in_=K.row[0:1, :]),
                   reads=[K.ROW], writes=[K.GSC[l][gi]])
        else:
            ci = {0: 0, 1: 1, 3: 2, 4: 3}[vec]
            mm(K, [(K.pm[:, q:q + 1], [(K.row[0:1, q * 128:(q + 1) * 128], K.ones_f[0:1, 0:1])]) for q in range(4)],
               reads=[K.CONST, K.ROW], writes=[K.PM])
            tr.op("act", lambda e: e.copy(out=K.modc_all[:, l, ci, half * 4:half * 4 + 4], in_=K.pm[:, 0:4]),
                  reads=[K.PM], writes=[MC])

    def finish():
        for (ci, ao, k) in ((1, 4, 0), (3, 5, 2)):
            gcol = (l * 4 + k) * 8
            tr.op("dve", (lambda ci=ci, ao=ao: (lambda e: e.tensor_scalar(
                out=K.modc_all[:, l, ao, :], in0=K.modc_all[:, l, ci, :], scalar1=1.0, scalar2=None, op0=ALU.add)))(),
                reads=[MC], writes=[MC])
            tr.op("dve", (lambda ao=ao, gcol=gcol: (lambda e: e.tensor_tensor(
                out=K.modc_all[:, l, ao, :], in0=K.modc_all[:, l, ao, :], in1=K.ngfm[:, gcol:gcol + 8], op=ALU.mult)))(),
                reads=[MC, K.CONST], writes=[MC])

    for j in range(12):
        tasks.append((lambda j=j: block(j)))
    tasks.append(finish)
    return tasks


def adaln_fetch(K, l):
    tr = K.tr
    K.modc = K.modc_all[:, l]
    K.MODC = K.MODCL[l]
    for gi in range(2):
        tr.dma("sp", (lambda gi=gi: (lambda e: e.dma_start(out=K.gg[gi][:], in_=K.gsc[l, gi:gi + 1, :].to_broadcast([128, D]))))(),
               reads=[K.GSC[l][gi]], writes=[K.GG[gi]])


def load_x(K, src_ap, srcB, blk, row0=None):
    i = K.xrot
    K.xrot = (i + 1) % 2
    r0 = blk * 128 if row0 is None else row0
    t = K.xb[i]
    K.tr.dma("sp", lambda e: e.dma_start(out=t[:], in_=src_ap[r0:r0 + 128, :]),
             reads=([srcB[blk]] if srcB is not None else []), writes=[K.XB[i]])
    return t, K.XB[i]


def rstd_of(K, in_ap, inB, n, mean_col=None, extra_reads=()):
    tr = K.tr
    i = K.strot
    K.strot = (i + 1) % 2
    st, STB = K.st4[i], K.ST4[i]
    sc = float(n) ** -0.5

    tr.op("act", lambda e: e.activation(out=K.junk[:, 0:n], in_=in_ap, func=AF.Square, scale=sc, accum_out=st[:, 0:1]),
          reads=[inB] + list(extra_reads), writes=[K.JUNK, STB])
    tr.op("act", lambda e: e.activation(out=st[:, 1:2], in_=st[:, 0:1], func=AF.Sqrt, bias=K.epsT[:, 0:1], scale=1.0),
          reads=[STB, K.CONST], writes=[STB])
    tr.op("dve", lambda e: e.reciprocal(out=st[:, 2:3], in_=st[:, 1:2]), reads=[STB], writes=[STB])
    return st, STB


def norm_in_seq(K, items, aidx, hT=None):
    for st_ in norm_in_steps(K, items, aidx, hT):
        st_()


def norm_in_steps(K, items, aidx, hT=None):
    tr = K.tr
    sh = 0 if aidx == 4 else 2
    n = len(items)
    xns = []
    modc, MODC = K.modc, K.MODC
    hT = K.hT if hT is None else hT

    def stage1(i):
        load_fn, hcol, HB = items[i]
        xt, XB = load_fn()
        st, STB = rstd_of(K, xt[:], XB, D)
        j = K.xnrot
        K.xnrot = 1 - j
        xn, XNB = K.xn2[j], K.XN2[j]
        tr.op("dve", lambda e: e.tensor_scalar(out=xn[:], in0=xt[:], scalar1=st[:, 2:3], scalar2=None, op0=ALU.mult),
              reads=[XB, STB], writes=[XNB])
        xns.append((xn, XNB))

    def stage2(i):
        load_fn, hcol, HB = items[i]
        xn, XNB = xns[i]

        def tps(e):
            for c in range(8):
                ins = e.transpose(out=K.pT[:, c, :], in_=xn[:, c * 128:(c + 1) * 128], identity=K.idb[:])
            return ins
        tr.op("pe", tps, reads=[XNB, K.CONST], writes=[K.PT])

        def mod_a(e):
            for c in range(0, 5):
                ins = e.activation(out=hT[:, c, hcol:hcol + 128], in_=K.pT[:, c, :], func=AF.Identity,
                                   scale=modc[:, aidx, c:c + 1], bias=modc[:, sh, c:c + 1])
            return ins

        def mod_d(e):
            for c in range(5, 8):
                ins = e.tensor_scalar(out=hT[:, c, hcol:hcol + 128], in0=K.pT[:, c, :], scalar1=modc[:, aidx, c:c + 1],
                                      scalar2=modc[:, sh, c:c + 1], op0=ALU.mult, op1=ALU.add)
            return ins
        tr.op("act", mod_a, reads=[K.PT, MODC], writes=[HB])
        tr.op("dve", mod_d, reads=[K.PT, MODC], writes=[HB])

    def mk(i):
        def step():
            if i < n:
                stage1(i)
            if i >= 1:
                stage2(i - 1)
        return step
    return [mk(i) for i in range(n + 1)]


def norm_out(K, xt, XB, gi, dst_ap, dstB, blk):
    tr = K.tr
    pyv, PYB = K.cur_py
    st, STB = rstd_of(K, pyv, PYB[0], D, extra_reads=PYB[1:])
    tr.op("dve", lambda e: e.tensor_tensor(out=K.tmpf[:], in0=pyv, in1=K.gg[gi][:], op=ALU.mult),
          reads=list(PYB) + [K.GG[gi]], writes=[K.TMPF])
    tr.op("dve", lambda e: e.scalar_tensor_tensor(out=xt[:], in0=K.tmpf[:], scalar=st[:, 2:3], in1=xt[:],
                                                   op0=ALU.mult, op1=ALU.add),
          reads=[K.TMPF, STB, XB], writes=[XB])
    tr.dma("sp", lambda e: e.dma_start(out=dst_ap[blk * 128:(blk + 1) * 128, :], in_=xt[:]),
           reads=[XB], writes=[dstB[blk]])


def final_proj(K, actT, ACTB, kchunks, bcol, reads_extra=(), wsel=None):
    K.pyrot = 1 - getattr(K, "pyrot", 1)
    if K.pyrot == 0:
        pyv, PYB = K.py[:, :], [K.PY]
    else:
        pyv, PYB = K.pab[:, 0:2, :].rearrange("p a b -> p (a b)"), K.PAB[0:2]
    K.cur_py = (pyv, PYB)
    groups = []
    for half in range(2):
        groups.append((pyv[:, half * 512:(half + 1) * 512],
                       [(actT[:, k, bcol:bcol + 128], (wsel(k) if wsel else K.wbig[:, k, :])[:, half * 512:(half + 1) * 512])
                        for k in range(kchunks)]))
    mm(K, groups, reads=list(ACTB) + [K.WBIG] + list(reads_extra), writes=PYB)


def ffn(K, l, src_ap, srcB, dst_ap, dstB):
    tr, dr = K.tr, K.dr
    K.lreset()
    hTb = K.larena("hTb", [128, 8, 1024], BF16)
    HTb = tr.bufs(8, "HTb")
    hbuf = [(K.hT, K.HT), (hTb, HTb)]
    load_big(K, dr["ffn_w_out"][K.lpos[l]], 22)
    win = dr["ffn_w_in"][K.lpos[l]]

    def nsteps(s):
        hT, HB = hbuf[s % 2]
        return norm_in_steps(K, [((lambda blk=s * SB + b: load_x(K, src_ap, srcB, blk)), b * 128, HB[b]) for b in range(SB)], 5, hT)

    for st_ in nsteps(0):
        st_()
    for s in range(NST):
        hT, HB = hbuf[s % 2]
        nxt = nsteps(s + 1) if s + 1 < NST else []
        for pc in range(11):
            wt, WB = wload(K, win[:, pc * 256:(pc + 1) * 256], 256)
            wu, WU = wload(K, win[:, DFF + pc * 256:DFF + (pc + 1) * 256], 256)
            for fc in range(2):
                f = pc * 2 + fc
                for tt in range(2):
                    pbank = (fc * 2 + tt) % 2 * 2
                    pg, pu = K.pab[:, pbank, 0:TT], K.pab[:, pbank + 1, 0:TT]
                    hs = slice(tt * TT, (tt + 1) * TT)
                    mm(K, [(pg, [(wt[:, k, fc * 128:(fc + 1) * 128], hT[:, k, hs]) for k in range(8)]),
                           (pu, [(wu[:, k, fc * 128:(fc + 1) * 128], hT[:, k, hs]) for k in range(8)])],
                       reads=[WB, WU] + HB[0:SB], writes=[K.PAB[pbank], K.PAB[pbank + 1]])
                    i = K.sgrot
                    K.sgrot = (i + 1) % 2
                    sgt, SGB = K.sg[i], K.SG[i]
                    tr.op("act", (lambda sgt=sgt, pg=pg: (lambda e: e.activation(out=sgt[:, 0:TT], in_=pg, func=AF.Silu)))(),
                          reads=[K.PAB[pbank]], writes=[SGB])
                    tr.op("dve", (lambda sgt=sgt, pu=pu, f=f, hs=hs: (lambda e: e.tensor_tensor(
                        out=K.act[:, f, hs], in0=pu, in1=sgt[:, 0:TT], op=ALU.mult)))(),
                        reads=[K.PAB[pbank + 1], SGB], writes=[K.ACT[f]])
            if 1 <= pc and pc - 1 < len(nxt):
                nxt[pc - 1]()
        for b in range(SB):
            blk = s * SB + b
            final_proj(K, K.act, K.ACT, 22, b * 128)
            xt, XB = load_x(K, src_ap, srcB, blk)
            norm_out(K, xt, XB, 1, dst_ap, dstB, blk)


def mixer2(K, l, src_ap, srcB, dst_ap, dstB):
    tr, dr, sb = K.tr, K.dr, K.larena
    K.lreset()
    if True:
        K.cw = sb("cw", [128, 24])
        K.ybuf = [sb("ybuf%d" % i, [128, 2 + TT]) for i in range(2)]
        K.YBUF = tr.bufs(2, "YBUF")
        K.carry = sb("carry", [128, 8, 2])
        K.CARRY = tr.buf("CARRY")
        K.cacc = sb("cacc", [128, TT])
        K.CACC = tr.buf("CACC")
    tr.dma("sp", lambda e: e.dma_start(out=K.cw[:], in_=dr["sc_cw"]), writes=[K.CONST])
    tr.op("dve", lambda e: e.memset(K.carry[:], 0.0), writes=[K.CARRY])
    load_big(K, dr["sc_w_out"], 8)
    win = dr["sc_w_in"]
    yrot = 0
    hTb = sb("hTb2", [128, 8, 1024], BF16)
    HTb = tr.bufs(8, "HTb2_")
    hbuf = [(K.hT, K.HT), (hTb, HTb)]

    def nsteps(s):
        hT_, HB_ = hbuf[s % 2]
        return norm_in_steps(K, [((lambda blk=s * SB + b: load_x(K, src_ap, srcB, blk)), b * 128, HB_[b]) for b in range(SB)], 4, hT_)

    for st_ in nsteps(0):
        st_()
    for s in range(NST):
        hT, HB = hbuf[s % 2]
        nxt = nsteps(s + 1) if s + 1 < NST else []
        for c in range(8):
            wt, WB = wload(K, win[:, c * 128:(c + 1) * 128], 128)
            w2_, WB2 = None, None
            wgc, WGC = wload(K, win[:, D + c * 128:D + (c + 1) * 128], 128)
            wxt, WXT = None, None
            for tt in range(2):
                hs = slice(tt * TT, (tt + 1) * TT)
                if tt == 0:
                    wxt, WXT = wload_extra(K, win[:, 2 * D + c * 128:2 * D + (c + 1) * 128])
                if (c * 2 + tt) % 2 == 0:
                    pgb, pgc, pxt = K.pab[:, 0, 0:TT], K.pab[:, 1, 0:TT], K.pab[:, 2, 0:TT]
                    PGB, PGC, PXT = K.PAB[0], K.PAB[1], K.PAB[2]
                else:
                    pgb, pgc, pxt = K.pab[:, 3, 0:TT], K.py[:, 0:TT], K.py[:, 512:512 + TT]
                    PGB, PGC, PXT = K.PAB[3], K.PY, K.PY
                mm(K, [(pgb, [(wt[:, k, 0:128], hT[:, k, hs]) for k in range(8)]),
                       (pgc, [(wgc[:, k, 0:128], hT[:, k, hs]) for k in range(8)]),
                       (pxt, [(wxt[:, k, 0:128], hT[:, k, hs]) for k in range(8)])],
                   reads=[WB, WGC, WXT] + HB[0:SB], writes=[PGB, PGC, PXT])
                yb, YB = K.ybuf[yrot], K.YBUF[yrot]
                yrot = 1 - yrot
                i = K.sgrot
                K.sgrot = (i + 1) % 2
                sgt, SGB = K.sg[i], K.SG[i]
                tr.op("act", (lambda sgt=sgt, pgc=pgc: (lambda e: e.copy(out=sgt[:, 0:TT], in_=pgc)))(),
                      reads=[PGC], writes=[SGB])

                CW = K.CONST
                steps = [
                    (lambda e, yb=yb, c=c: e.tensor_copy(out=yb[:, 0:2], in_=K.carry[:, c, :]), [K.CARRY], [YB]),
                    (lambda e, yb=yb, pxt=pxt, sgt=sgt: e.tensor_tensor(out=yb[:, 2:2 + TT], in0=pxt, in1=sgt[:, 0:TT], op=ALU.mult),
                     [PXT, SGB], [YB]),
                    (lambda e, yb=yb, c=c: e.tensor_copy(out=K.carry[:, c, :], in_=yb[:, TT:TT + 2]), [YB], [K.CARRY]),
                    (lambda e, yb=yb, c=c: e.tensor_scalar(out=K.cacc[:], in0=yb[:, 0:TT], scalar1=K.cw[:, c:c + 1],
                                                          scalar2=None, op0=ALU.mult), [YB, CW], [K.CACC]),
                    (lambda e, yb=yb, c=c: e.scalar_tensor_tensor(out=K.cacc[:], in0=yb[:, 1:1 + TT], scalar=K.cw[:, 8 + c:9 + c],
                                                                 in1=K.cacc[:], op0=ALU.mult, op1=ALU.add), [YB, CW, K.CACC], [K.CACC]),
                    (lambda e, yb=yb, c=c: e.scalar_tensor_tensor(out=K.cacc[:], in0=yb[:, 2:2 + TT], scalar=K.cw[:, 16 + c:17 + c],
                                                                 in1=K.cacc[:], op0=ALU.mult, op1=ALU.add), [YB, CW, K.CACC], [K.CACC]),
                    (lambda e, pgb=pgb, c=c, hs=hs: e.tensor_tensor(out=K.act[:, c, hs], in0=pgb, in1=K.cacc[:], op=ALU.mult),
                     [PGB, K.CACC], [K.ACT[c]]),
                ]
                for fn, rd, wr in steps:
                    tr.op("dve", fn, reads=rd, writes=wr)
            if c < len(nxt):
                nxt[c]()
        for b in range(SB):
            blk = s * SB + b
            final_proj(K, K.act, K.ACT[0:8], 8, b * 128)
            xt, XB = load_x(K, src_ap, srcB, blk)
            norm_out(K, xt, XB, 0, dst_ap, dstB, blk)


def wload_extra(K, dram_ap):
    return wload(K, dram_ap, 128)


def mixer0(K, l, dst_ap, dstB):
    tr, dr, sb, nc = K.tr, K.dr, K.larena, K.nc
    K.lreset()
    kT_d = nc.dram_tensor("kT_d", [8, 128, KS], BF16, kind="Internal").ap()
    qT_d = nc.dram_tensor("qT_d", [8, 128, KS], BF16, kind="Internal").ap()
    v_d = nc.dram_tensor("v_d", [8, KS, 128], BF16, kind="Internal").ap()
    KD, QD, VD = tr.bufs(8, "KD"), tr.bufs(8, "QD"), tr.bufs(8, "VD")
    valid = sb("valid", [128, 32])
    maskb = sb("maskb", [128, 3, 384], BF16)
    ntri = sb("ntri", [128, 128], BF16)
    nones = sb("nones", [128, 128], BF16)
    trif = sb("trif", [128, 128])
    kTa = sb("kTa", [128, KS], BF16)
    qTa = [sb("qTa%d" % i, [128, W], BF16) for i in range(2)]
    vca = sb("vca", [128, 32, 128], BF16)
    esb = sb("esb", [128, 2, 384])
    Sf = sb("Sf", [128, 2, 384])
    mark_b = K.aoff[BF16]
    qst = sb("qst", [128, 1024], BF16)
    kst = sb("kst", [128, 1024], BF16)
    vst = sb("vst", [128, 8, 128], BF16)
    K.aoff[BF16] = mark_b
    spb = [sb("spb%d" % i, [128, 2, 384], BF16) for i in range(3)]
    abb = [sb("abb%d" % i, [128, 2, 384], BF16) for i in range(2)]
    Sb = [sb("Sb%d" % i, [128, 2, 384], BF16) for i in range(3)]
    G0 = tr.buf("G0")
    QST, KST, VST, KTA, VCA = (tr.buf(n) for n in ("QST", "KST", "VST", "KTA", "VCA"))
    ESB, SF = tr.buf("ESB"), tr.buf("SF")
    SPB, ABB, SBB = tr.bufs(3, "SPB"), tr.bufs(2, "ABB"), tr.bufs(3, "SBB")
    tr.dma("sp", lambda e: e.dma_start(out=valid[:], in_=dr["valid"]), writes=[G0])
    tr.dma("sp", lambda e: e.dma_start(out=trif[:], in_=dr["sb_tri"]), writes=[G0])
    for j in range(3):
        tr.dma("sp", (lambda j=j: (lambda e: e.dma_start(out=K.tmpf[:, 0:384], in_=dr["sb_mask"][:, j, :])))(), writes=[K.TMPF])
        tr.op("dve", (lambda j=j: (lambda e: e.tensor_copy(out=maskb[:, j, :], in_=K.tmpf[:, 0:384])))(), reads=[K.TMPF], writes=[G0])
    tr.op("dve", lambda e: e.tensor_scalar(out=ntri[:], in0=trif[:], scalar1=-1.0, scalar2=None, op0=ALU.mult), reads=[G0], writes=[G0])
    tr.op("dve", lambda e: e.memset(nones[:], -1.0), writes=[G0])
    QTA = tr.buf("QTA")
    tr.op("pool", lambda e: e.memset(qTa[0][64:128, :], 0.0), writes=[QTA])
    tr.op("pool", lambda e: e.memset(qTa[1][0:64, :], 0.0), writes=[QTA])
    wqkv = dr["sb_w_qkv"]
    for g in range(4):
        norm_in_seq(K, [((lambda blk=g * 8 + b: load_x(K, dr["xseq"], None, blk)), b * 128, K.HT[b]) for b in range(8)], 4)
        for c in range(8):
            wq, WQ = wload(K, wqkv[:, c * 128:(c + 1) * 128], 128)
            wk, WK = wload(K, wqkv[:, D + c * 128:D + (c + 1) * 128], 128)
            wv, WV = wload_extra(K, wqkv[:, 2 * D + c * 128:2 * D + (c + 1) * 128])
            for tt in range(2):
                hs = slice(tt * 512, (tt + 1) * 512)
                pq, pk = K.pab[:, 0, :], K.pab[:, 1, :]
                mm(K, [(pq, [(wq[:, k, 0:128], K.hT[:, k, hs]) for k in range(8)]),
                       (pk, [(wk[:, k, 0:128], K.hT[:, k, hs]) for k in range(8)])],
                   reads=[WQ, WK] + K.HT, writes=K.PAB[0:2])
                tr.op("act", (lambda hs=hs, pq=pq: (lambda e: e.activation(out=qst[:, hs], in_=pq, func=AF.Identity, scale=0.125)))(),
                      reads=[K.PAB[0]], writes=[QST])
                tr.op("dve", (lambda hs=hs, pk=pk: (lambda e: e.tensor_copy(out=kst[:, hs], in_=pk)))(),
                      reads=[K.PAB[1]], writes=[KST])
            gs = slice(g * 1024, (g + 1) * 1024)
            tr.dma("sp", (lambda c=c, gs=gs: (lambda e: e.dma_start(out=qT_d[c, :, gs], in_=qst[:])))(), reads=[QST], writes=[QD[c]])
            tr.dma("sp", (lambda c=c, gs=gs: (lambda e: e.dma_start(out=kT_d[c, :, gs], in_=kst[:])))(), reads=[KST], writes=[KD[c]])
            mm(K, [(K.pab[:, 2 + b // 4, (b % 4) * 128:(b % 4 + 1) * 128],
                    [(K.hT[:, k, b * 128:(b + 1) * 128], wv[:, k, 0:128]) for k in range(8)]) for b in range(8)],
               reads=[WV] + K.HT, writes=K.PAB[2:4])
            for b in range(8):
                tr.op("dve", (lambda b=b, g=g: (lambda e: e.tensor_scalar(
                    out=vst[:, b, :], in0=K.pab[:, 2 + b // 4, (b % 4) * 128:(b % 4 + 1) * 128],
                    scalar1=valid[:, g * 8 + b:g * 8 + b + 1], scalar2=None, op0=ALU.mult)))(),
                    reads=[K.PAB[2 + b // 4], G0], writes=[VST])
            tr.dma("sp", (lambda c=c, gs=gs: (lambda e: e.dma_start(
                out=v_d[c, gs, :].rearrange("(b p) i -> p b i", p=128), in_=vst[:])))(), reads=[VST], writes=[VD[c]])
    tr.barrier()
    oT = K.act[:].rearrange("p a b -> p (a b)").rearrange("p (c t) -> p c t", t=W)
    pz = K.pab[:, 0:2, 0:384]
    prs2 = [K.pab[:, 2:4, 0:384], K.pm2[:, 0:2, 0:384]]
    PRL2 = [K.PAB[2:4], [K.PT, K.PM]]
    po = K.py[:, :].rearrange("p (c t) -> p c t", t=512)[:, :, 0:384]
    PZL, POL = K.PAB[0:2], [K.PY]
    hps = [slice(0, 64), slice(64, 128)]
    units = []
    for qt in range(7):
        kbmax = 13 + 3 * qt
        for kb in range(kbmax, -1, -1):
            units.append((qt, kb, kbmax - kb))
    nU = len(units)

    def S1(c, t):
        qt, kb, j = units[t]
        qs = slice(qt * 384, (qt + 1) * 384)
        sp, SPt = spb[t % 3], SPB[t % 3]
        mm(K, [(pz[:, ch, :], [(kTa[:, kb * 128:(kb + 1) * 128], qTa[ch][:, qs])]) for ch in range(2)],
           reads=[KTA, QTA], writes=PZL)
        tr.op("act", lambda e: e.activation(out=esb[:], in_=pz, func=AF.Exp), reads=PZL, writes=[ESB])
        tr.op("act", lambda e: e.activation(out=sp[:], in_=esb[:], func=AF.Ln, bias=1.0, scale=1.0), reads=[ESB], writes=[SPt])
        if j <= 2:
            for ch in range(2):
                tr.op("dve", (lambda ch=ch: (lambda e: e.tensor_tensor(out=sp[:, ch, :], in0=sp[:, ch, :], in1=maskb[:, 2 - j, :], op=ALU.mult)))(),
                      reads=[SPt, G0], writes=[SPt])
        if kb > 0:
            if j == 0:
                tr.op("dve", lambda e: e.tensor_copy(out=Sf[:], in_=sp[:]), reads=[SPt], writes=[SF])
            else:
                tr.op("dve", lambda e: e.tensor_tensor(out=Sf[:], in0=Sf[:], in1=sp[:], op=ALU.add), reads=[SPt, SF], writes=[SF])
            nb = (t + 1) % 3
            tr.op("dve", lambda e: e.tensor_copy(out=Sb[nb][:], in_=Sf[:]), reads=[SF], writes=[SBB[nb]])

    def S2(c, t):
        qt, kb, j = units[t]
        qs = slice(qt * 384, (qt + 1) * 384)
        sp, SPt = spb[t % 3], SPB[t % 3]
        ab, ABt = abb[t % 2], ABB[t % 2]
        groups = []
        rd = [KTA, QTA, G0, SPt]
        pr, PRL = prs2[t % 2], PRL2[t % 2]
        for ch in range(2):
            pairs = [(kTa[:, kb * 128:(kb + 1) * 128], qTa[ch][:, qs]), (ntri[:], sp[:, ch, :])]
            if j > 0:
                pairs.append((nones[:], Sb[t % 3][:, ch, :]))
            groups.append((pr[:, ch, :], pairs))
        if j > 0:
            rd.append(SBB[t % 3])
        mm(K, groups, reads=rd, writes=PRL)
        tr.op("act", lambda e: e.activation(out=ab[:], in_=pr, func=AF.Exp), reads=PRL, writes=[ABt])
        if j <= 2:
            for ch in range(2):
                tr.op("dve", (lambda ch=ch: (lambda e: e.tensor_tensor(out=ab[:, ch, :], in0=ab[:, ch, :], in1=maskb[:, 2 - j, :], op=ALU.mult)))(),
                      reads=[ABt, G0], writes=[ABt])

    def S3(c, t):
        qt, kb, j = units[t]
        qs = slice(qt * 384, (qt + 1) * 384)
        ab, ABt = abb[t % 2], ABB[t % 2]

        def av(e):
            for ch in range(2):
                ins = e.matmul(po[:, ch, :], lhsT=vca[:, kb, :], rhs=ab[:, ch, :], start=(j == 0), stop=(kb == 0))
            return ins
        tr.op("pe", av, reads=[VCA, ABt], writes=POL)
        if kb == 0:
            for ch in range(2):
                tr.op("dve", (lambda ch=ch: (lambda e: e.tensor_copy(out=oT[hps[ch], c, qs], in_=po[hps[ch], ch, :])))(),
                      reads=POL, writes=[K.ACT[0]])

    for c in range(8):
        tr.dma("sp", (lambda c=c: (lambda e: e.dma_start(out=kTa[:], in_=kT_d[c, :, :])))(), reads=[KD[c]], writes=[KTA])
        for ch in range(2):
            tr.dma("sp", (lambda c=c, ch=ch: (lambda e: e.dma_start(out=qTa[ch][hps[ch], :], in_=qT_d[c, hps[ch], HALO:KS])))(),
                   reads=[QD[c]], writes=[QTA])
        for q4 in range(4):
            tr.dma("sp", (lambda c=c, q4=q4: (lambda e: e.dma_start(
                out=vca[:, q4 * 8:(q4 + 1) * 8, :], in_=v_d[c, q4 * 1024:(q4 + 1) * 1024, :].rearrange("(b p) i -> p b i", p=128))))(),
                reads=[VD[c]], writes=[VCA])
        for t in range(nU + 2):
            if t % 16 == 8 and K.bg:
                K.bg.pop(0)()
            if t < nU:
                S1(c, t)
            if t >= 2:
                S3(c, t - 2)
            if 1 <= t <= nU:
                S2(c, t - 1)
    while K.bg:
        K.bg.pop(0)()
    load_big(K, dr["sb_w_o"], 8)
    for blk in range(NBLK):
        final_proj(K, oT, [K.ACT[0]], 8, blk * 128)
        xt, XB = load_x(K, dr["xseq"], None, blk, row0=HALO + blk * 128)
        norm_out(K, xt, XB, 0, dst_ap, dstB, blk)


def gelu_to(K, src_ps, SRCB, n, out_ap, OUTB):
    tr = K.tr
    i = K.grot
    K.grot = 1 - i
    (t1, T1), (t2, T2) = K.gtmp[i]
    tr.op("act", lambda e: e.activation(out=t1[:, 0:n], in_=src_ps, func=AF.Square, scale=0.044715 ** 0.5),
          reads=[SRCB], writes=[T1])
    tr.op("dve", lambda e: e.scalar_tensor_tensor(out=t1[:, 0:n], in0=t1[:, 0:n], scalar=1.0, in1=src_ps,
                                                   op0=ALU.add, op1=ALU.mult), reads=[T1, SRCB], writes=[T1])
    tr.op("act", lambda e: e.activation(out=t2[:, 0:n], in_=t1[:, 0:n], func=AF.Sigmoid, scale=1.5957691216057308),
          reads=[T1], writes=[T2])
    tr.op("dve", lambda e: e.tensor_tensor(out=out_ap, in0=src_ps, in1=t2[:, 0:n], op=ALU.mult),
          reads=[T2, SRCB], writes=[OUTB])


def mixer1(K, l, src_ap, srcB, dst_ap, dstB):
    tr, dr, sb = K.tr, K.dr, K.larena
    K.lreset()
    if True:
        K.wsTf = K.tmpf[:].rearrange("p (g i) -> p g i", i=128)
        gt = [sb("gtmp%d" % i, [128, 512]) for i in range(2)]
        GT = tr.bufs(2, "GTMP")
        K.gtmp = [((K.sg[0], K.SG[0]), (K.sg[1], K.SG[1])), ((gt[0], GT[0]), (gt[1], GT[1]))]
        K.grot = 0
        K.wo_ar = sb("wo_ar", [128, 10, 1024], BF16)
        K.WOA = tr.buf("WOA")
        K.m01 = sb("m01", [128, 128])
        K.wsT = sb("wsT", [128, 8, 128], BF16)
        K.lngb = sb("lngb", [128, 2048], BF16)
        K.biasB = sb("biasB", [128, 8, 128])
        K.vbf2 = [sb("vbf%d" % i, [128, 2048], BF16) for i in range(2)]
        K.VBF2 = tr.bufs(2, "VBF2_")
        K.vh = sb("vh", [128, 2048], BF16)
        K.bst = sb("bst", [128, 4, 6])
        K.mv = sb("mv", [128, 4])
        K.G1 = tr.buf("G1C")
        K.VBF, K.VH, K.BST, K.MV = tr.buf("VBF"), tr.buf("VH"), tr.buf("BST"), tr.buf("MV")
    G1 = K.G1
    tr.dma("sp", lambda e: e.dma_start(out=K.wsTf, in_=dr["sg_wsT"]), writes=[K.TMPF])
    tr.dma("sp", lambda e: e.dma_start(out=K.m01[:], in_=dr["sg_m01"]), writes=[G1])
    tr.dma("sp", lambda e: e.dma_start(out=K.biasB.rearrange("p g i -> p (g i)"), in_=dr["sg_bias"][0:1, :].to_broadcast([128, 1024])),
           writes=[G1])
    for g in range(8):
        tr.op("dve", (lambda g=g: (lambda e: e.tensor_tensor(out=K.wsT[:, g, :], in0=K.wsTf[:, g, :], in1=K.m01[:], op=ALU.mult)))(),
              reads=[G1, K.TMPF], writes=[G1])
    for q in range(4):
        tr.dma("sp", (lambda q=q: (lambda e: e.dma_start(out=K.row[:], in_=dr["sg_lng"][0:1, q * 512:(q + 1) * 512])))(),
               writes=[K.ROW])
        mm(K, [(K.pm[:, :], [(K.ones_f[0:1, :], K.row[0:1, :])])], reads=[K.CONST, K.ROW], writes=[K.PM])
        tr.op("act", (lambda q=q: (lambda e: e.copy(out=K.lngb[:, q * 512:(q + 1) * 512], in_=K.pm[:, :])))(),
              reads=[K.PM], writes=[G1])
    win = dr["sg_w_in"]
    wv = K.wbig[:, 0:16, :].rearrange("p a b -> p (a b)").rearrange("p (k n) -> p k n", n=2048)
    pv = K.pab[:, :, :]
    psv = K.py[:, :].rearrange("p (c i) -> p c i", i=128)
    for k0 in range(0, 8, 2):
        v = wv[:, k0:k0 + 2, :]
        srcw = win[k0 * 128:(k0 + 2) * 128, 2048:4096].rearrange("(c p) n -> p c n", p=128)
        tr.dma("pool", (lambda v=v, srcw=srcw: (lambda e: e.dma_start(out=v, in_=srcw)))(), writes=[K.WBIG])
    for (k0, k1) in ((0, 3), (3, 6)):
        v = K.wbig[:, 16 + k0:16 + k1, :]
        srcw = dr["sg_w_out"][k0 * 128:k1 * 128, :].rearrange("(c p) n -> p c n", p=128)
        tr.dma("pool", (lambda v=v, srcw=srcw: (lambda e: e.dma_start(out=v, in_=srcw)))(), writes=[K.WBIG])
    for (k0, k1) in ((6, 10), (10, 13), (13, 16)):
        v = K.wo_ar[:, k0 - 6:k1 - 6, :]
        srcw = dr["sg_w_out"][k0 * 128:k1 * 128, :].rearrange("(c p) n -> p c n", p=128)
        tr.dma("pool", (lambda v=v, srcw=srcw: (lambda e: e.dma_start(out=v, in_=srcw)))(), writes=[K.WOA])
    for s in range(NST):
        norm_in_seq(K, [((lambda blk=s * SB + b: load_x(K, src_ap, srcB, blk)), b * 128, K.HT[b]) for b in range(SB)], 4)
        for pc in range(4):
            wt, WB = wload(K, win[:, pc * 512:(pc + 1) * 512], 512)
            for fc in range(4):
                f = pc * 4 + fc
                for tt in range(2):
                    bank = (fc * 2 + tt) % 4
                    pg = K.pab[:, bank, 0:TT]
                    hs = slice(tt * TT, (tt + 1) * TT)
                    mm(K, [(pg, [(wt[:, k, fc * 128:(fc + 1) * 128], K.hT[:, k, hs]) for k in range(8)])],
                       reads=[WB] + K.HT, writes=[K.PAB[bank]])
                    gelu_to(K, pg, K.PAB[bank], TT, K.act[:, f, hs], K.ACT[f])
        def stageA(b):
            bs = slice(b * 128, (b + 1) * 128)
            vb, VB_ = K.vbf2[b % 2], K.VBF2[b % 2]
            for n in range(4):
                mm(K, [(pv[:, n, :], [(K.hT[:, k, bs], wv[:, k, n * 512:(n + 1) * 512]) for k in range(8)])],
                   reads=[K.WBIG, K.HT[b]], writes=[K.PAB[n]])
            for n in range(4):
                gelu_to(K, pv[:, n, :], K.PAB[n], 512, vb[:, n * 512:(n + 1) * 512], VB_)

        def stageB(b):
            bs = slice(b * 128, (b + 1) * 128)
            vb, VB_ = K.vbf2[b % 2], K.VBF2[b % 2]
            for n in range(4):
                tr.op("dve", (lambda n=n: (lambda e: e.bn_stats(out=K.bst[:, n, :], in_=vb[:, n * 512:(n + 1) * 512])))(),
                      reads=[VB_], writes=[K.BST])
            tr.op("dve", lambda e: e.bn_aggr(out=K.mv[:, 0:2], in_=K.bst[:].rearrange("p a b -> p (a b)")), reads=[K.BST], writes=[K.MV])
            tr.op("act", lambda e: e.activation(out=K.mv[:, 2:3], in_=K.mv[:, 1:2], func=AF.Sqrt, bias=K.epsT[:, 0:1], scale=1.0),
                  reads=[K.MV, K.CONST], writes=[K.MV])
            tr.op("dve", lambda e: e.reciprocal(out=K.mv[:, 2:3], in_=K.mv[:, 2:3]), reads=[K.MV], writes=[K.MV])
            tr.op("dve", lambda e: e.tensor_scalar(out=K.mv[:, 3:4], in0=K.mv[:, 0:1], scalar1=K.mv[:, 2:3], scalar2=-1.0,
                                                    op0=ALU.mult, op1=ALU.mult), reads=[K.MV], writes=[K.MV])
            for hf in range(2):
                cs = slice(hf * 1024, (hf + 1) * 1024)
                tr.op("act", (lambda cs=cs: (lambda e: e.activation(out=K.tmpf[:], in_=vb[:, cs], func=AF.Identity,
                                                                    scale=K.mv[:, 2:3], bias=K.mv[:, 3:4])))(),
                      reads=[VB_, K.MV], writes=[K.TMPF])
                tr.op("dve", (lambda cs=cs: (lambda e: e.tensor_tensor(out=K.vh[:, cs], in0=K.tmpf[:], in1=K.lngb[:, cs], op=ALU.mult)))(),
                      reads=[K.TMPF, G1], writes=[K.VH])
            for hf in range(2):
                groups = []
                for cc in range(8):
                    ch = hf * 8 + cc
                    g = ch // 2
                    groups.append((psv[:, cc, :], [(K.vh[:, ch * 128:(ch + 1) * 128], K.wsT[:, g, :])]))
                mm(K, groups, reads=[K.VH, G1], writes=[K.PY])
                tmp4 = K.tmpf[:].rearrange("p (c i) -> p c i", i=128)
                for par in range(2):
                    tr.op("dve", (lambda hf=hf, par=par: (lambda e: e.tensor_tensor(
                        out=tmp4[:, par:8:2, :], in0=psv[:, par:8:2, :], in1=K.biasB[:, 4 * hf:4 * hf + 4, :], op=ALU.add)))(),
                        reads=[K.PY, G1], writes=[K.TMPF])
                tr.op("dve", (lambda hf=hf, bs=bs: (lambda e: e.tensor_tensor(
                    out=K.act[:, hf * 8:(hf + 1) * 8, bs], in0=tmp4, in1=K.act[:, hf * 8:(hf + 1) * 8, bs], op=ALU.mult)))(),
                    reads=[K.TMPF] + K.ACT[hf * 8:(hf + 1) * 8], writes=K.ACT[hf * 8:(hf + 1) * 8])

        for b in range(SB + 1):
            if b < SB:
                stageA(b)
            if b >= 1:
                stageB(b - 1)
        for b in range(SB):
            blk = s * SB + b
            final_proj(K, K.act, K.ACT[0:16], 16, b * 128, reads_extra=[K.WOA],
                       wsel=lambda k: (K.wbig[:, 16 + k, :] if k < 6 else K.wo_ar[:, k - 6, :]))
            xt, XB = load_x(K, src_ap, srcB, blk)
            norm_out(K, xt, XB, 0, dst_ap, dstB, blk)


def mixer3(K, l, src_ap, srcB, dst_ap, dstB):
    tr, dr, sb = K.tr, K.dr, K.larena
    K.lreset()
    kcar = sb("kcar", [128, 8, 512], BF16)
    vcar = sb("vcar", [128, 8, 4, 128], BF16)
    qT2 = [sb("qT2_%d" % i, [128, ST], BF16) for i in range(2)]
    kTc = sb("kTc", [128, 1408], BF16)
    vc2 = [sb("vc2_%d" % i, [128, 11, 128], BF16) for i in range(2)]
    pTb = [sb("pTb%d" % i, [128, 2, 5, 128], BF16) for i in range(2)]
    Et = sb("Et", [128, 16, 2, 128], BF16)
    ones2 = [sb("ones2_%d" % i, [128, 128], BF16) for i in range(2)]
    pexp = [sb("pexp%d" % i, [128, 2, 2, 128]) for i in range(2)]
    bfar = sb("bfar", [128, 16])
    m04 = sb("m04", [128, 2, 128])
    rden = sb("rden", [128, 128])
    G3 = tr.buf("G3")
    KCAR, VCAR, QTC, KTC, VC, RDEN = (tr.buf(n) for n in ("KCAR", "VCAR", "QTC", "KTC", "VC", "RDEN"))
    PEXP, PTB = tr.bufs(2, "PEXP"), tr.bufs(2, "PTB")
    hps = [slice(0, 64), slice(64, 128)]
    tr.dma("sp", lambda e: e.dma_start(out=bfar[:], in_=dr["cb_bfar"]), writes=[G3])
    tr.dma("sp", lambda e: e.dma_start(out=m04[:], in_=dr["cb_m04"]), writes=[G3])
    for ch in range(2):
        o = 1 - ch
        tr.op("pool", (lambda ch=ch, o=o: (lambda e: e.memset(qT2[ch][hps[o], :], 0.0)))(), writes=[QTC])
        tr.op("pool", (lambda ch=ch, o=o: (lambda e: e.memset(vc2[ch][:, :, hps[o]], 0.0)))(), writes=[VC])
        tr.op("pool", (lambda ch=ch, o=o: (lambda e: e.memset(ones2[ch][:, hps[o]], 0.0)))(), writes=[G3])
        tr.op("pool", (lambda ch=ch: (lambda e: e.memset(ones2[ch][:, hps[ch]], 1.0)))(), writes=[G3])
    etf = K.tmpf[:].rearrange("p (h r i) -> p h r i", h=4, r=2)
    for q in range(4):
        tr.dma("sp", (lambda q=q: (lambda e: e.dma_start(out=etf, in_=dr["cb_bt"][:, 4 * q:4 * q + 4, :, :])))(), writes=[K.TMPF])
        tr.op("act", lambda e: e.activation(out=K.tmpf[:], in_=K.tmpf[:], func=AF.Exp), reads=[K.TMPF], writes=[K.TMPF])
        for hq in range(4):
            h = 4 * q + hq
            tr.op("dve", (lambda h=h, hq=hq: (lambda e: e.tensor_copy(out=Et[:, h, 0, :], in_=etf[:, hq, 0, :])))(),
                  reads=[K.TMPF], writes=[G3])
            tr.op("dve", (lambda h=h, hq=hq: (lambda e: e.tensor_tensor(out=Et[:, h, 1, :], in0=etf[:, hq, 1, :], in1=m04[:, 1, :], op=ALU.mult)))(),
                  reads=[K.TMPF, G3], writes=[G3])
    load_big(K, dr["cb_w_o"], 8)
    wqkv = dr["cb_w_qkv"]
    pz = K.pab[:].rearrange("p (ch a) b -> p ch (a b)", ch=2, a=2)[:, :, 0:640].rearrange("p ch (r i) -> p ch r i", i=128)
    po, pden = K.pm[:, 0:128], K.pm[:, 128:256]

    def S1(s, c, b):
        bs = slice(b * 128, (b + 1) * 128)
        r0 = max(0, 4 - (s * SB + b))
        sl = b % 2
        pt_, PT_ = pTb[sl], PTB[sl]
        pe_, PE_ = pexp[sl], PEXP[sl]
        mm(K, [(pz[:, ch, r, :], [(kTc[:, (b + r) * 128:(b + r + 1) * 128], qT2[ch][:, bs])])
               for ch in range(2) for r in range(r0, 5)], reads=[KTC, QTC], writes=K.PAB)
        if r0 <= 2:
            for ch in range(2):
                h = 2 * c + ch
                tr.op("act", (lambda ch=ch, h=h: (lambda e: e.activation(out=pt_[:, ch, r0:3, :], in_=pz[:, ch, r0:3, :], func=AF.Exp,
                                                                          bias=bfar[:, h:h + 1], scale=1.0)))(),
                      reads=K.PAB + [G3], writes=[PT_])
        rlo = max(r0, 3)
        tr.op("act", lambda e: e.activation(out=pe_[:, :, rlo - 3:2, :], in_=pz[:, :, rlo:5, :], func=AF.Exp), reads=K.PAB, writes=[PE_])
        if r0 == 0:
            for ch in range(2):
                tr.op("dve", (lambda ch=ch: (lambda e: e.tensor_tensor(out=pt_[:, ch, 0, :], in0=pt_[:, ch, 0, :], in1=m04[:, 0, :], op=ALU.mult)))(),
                      reads=[PT_, G3], writes=[PT_])
        if r0 <= 3:
            tr.op("dve", lambda e: e.tensor_tensor(out=pt_[:, :, 3, :], in0=pe_[:, :, 0, :], in1=Et[:, 2 * c:2 * c + 2, 0, :], op=ALU.mult),
                  reads=[PE_, G3], writes=[PT_])
        tr.op("dve", lambda e: e.tensor_tensor(out=pt_[:, :, 4, :], in0=pe_[:, :, 1, :], in1=Et[:, 2 * c:2 * c + 2, 1, :], op=ALU.mult),
              reads=[PE_, G3], writes=[PT_])

    def S2(s, c, b):
        bs = slice(b * 128, (b + 1) * 128)
        r0 = max(0, 4 - (s * SB + b))
        sl = b % 2
        pt_, PT_ = pTb[sl], PTB[sl]
        prs = [(ch, r) for ch in range(2) for r in range(r0, 5)]
        mm(K, [(po, [(vc2[ch][:, b + r, :], pt_[:, ch, r, :]) for ch, r in prs]),
               (pden, [(ones2[ch][:], pt_[:, ch, r, :]) for ch, r in prs])],
           reads=[VC, PT_, G3], writes=[K.PM])
        tr.op("dve", lambda e: e.reciprocal(out=rden[:], in_=pden), reads=[K.PM], writes=[RDEN])
        tr.op("dve", lambda e: e.tensor_tensor(out=K.act[:, c, bs], in0=po, in1=rden[:], op=ALU.mult),
              reads=[K.PM, RDEN], writes=[K.ACT[c]])

    for s in range(NST):
        norm_in_seq(K, [((lambda blk=s * SB + b: load_x(K, src_ap, srcB, blk)), b * 128, K.HT[b]) for b in range(SB)], 4)
        for c in range(8):
            wq, WQ = wload(K, wqkv[:, c * 128:(c + 1) * 128], 128)
            wk, WK = wload(K, wqkv[:, D + c * 128:D + (c + 1) * 128], 128)
            wv, WV = wload(K, wqkv[:, 2 * D + c * 128:2 * D + (c + 1) * 128], 128)
            if s > 0:
                tr.op("pool", (lambda c=c: (lambda e: e.tensor_copy(out=kTc[:, 0:512], in_=kcar[:, c, :])))(),
                      reads=[KCAR], writes=[KTC])
                for ch in range(2):
                    tr.op("pool", (lambda c=c, ch=ch: (lambda e: e.tensor_copy(out=vc2[ch][:, 0:4, hps[ch]], in_=vcar[:, c, :, hps[ch]])))(),
                          reads=[VCAR], writes=[VC])
            for tt in range(2):
                hs = slice(tt * TT, (tt + 1) * TT)
                pq, pk = K.pab[:, 0, 0:TT], K.pab[:, 1, 0:TT]
                mm(K, [(pq, [(wq[:, k, 0:128], K.hT[:, k, hs]) for k in range(8)]),
                       (pk, [(wk[:, k, 0:128], K.hT[:, k, hs]) for k in range(8)])],
                   reads=[WQ, WK] + K.HT[0:SB], writes=K.PAB[0:2])
                for ch in range(2):
                    tr.op("act", (lambda hs=hs, ch=ch: (lambda e: e.activation(out=qT2[ch][hps[ch], hs], in_=K.pab[hps[ch], 0, 0:TT],
                                                                               func=AF.Identity, scale=0.125)))(),
                          reads=[K.PAB[0]], writes=[QTC])
                tr.op("dve", (lambda tt=tt, pk=pk: (lambda e: e.tensor_copy(out=kTc[:, 512 + tt * TT:512 + (tt + 1) * TT], in_=pk)))(),
                      reads=[K.PAB[1]], writes=[KTC])
            mm(K, [(K.pab[:, 2 + b // 4, (b % 4) * 128:(b % 4 + 1) * 128],
                    [(K.hT[:, k, b * 128:(b + 1) * 128], wv[:, k, 0:128]) for k in range(8)]) for b in range(SB)],
               reads=[WV] + K.HT[0:SB], writes=K.PAB[2:4])
            for ch in range(2):
                tr.op("dve", (lambda ch=ch: (lambda e: e.tensor_copy(
                    out=vc2[ch][:, 4:8, hps[ch]], in_=K.pab[:, 2, :].rearrange("p (b i) -> p b i", i=128)[:, :, hps[ch]])))(),
                    reads=[K.PAB[2]], writes=[VC])
                tr.op("dve", (lambda ch=ch: (lambda e: e.tensor_copy(
                    out=vc2[ch][:, 8:11, hps[ch]], in_=K.pab[:, 3, 0:384].rearrange("p (b i) -> p b i", i=128)[:, :, hps[ch]])))(),
                    reads=[K.PAB[3]], writes=[VC])
            tr.op("pool", (lambda c=c: (lambda e: e.tensor_copy(out=kcar[:, c, :], in_=kTc[:, 896:1408])))(),
                  reads=[KTC], writes=[KCAR])
            for ch in range(2):
                tr.op("pool", (lambda c=c, ch=ch: (lambda e: e.tensor_copy(out=vcar[:, c, :, hps[ch]], in_=vc2[ch][:, 7:11, hps[ch]])))(),
                      reads=[VC], writes=[VCAR])
            for b in range(SB + 1):
                if b < SB:
                    S1(s, c, b)
                if b >= 1:
                    S2(s, c, b - 1)
        for b in range(SB):
            blk = s * SB + b
            final_proj(K, K.act, K.ACT[0:8], 8, b * 128)
            xt, XB = load_x(K, src_ap, srcB, blk)
            norm_out(K, xt, XB, 0, dst_ap, dstB, blk)


def _consts():
    c = {}
    c["ident"] = np.eye(128, dtype=np.float32)
    p = np.arange(128)[:, None]
    f = np.arange(384)[None, :]
    c["sb_mask"] = np.stack([(f > p + 128 * j) for j in range(3)], axis=1).astype(np.float32)
    j = np.arange(128)[:, None]
    s = np.arange(128)[None, :]
    c["sb_tri"] = (j >= s).astype(np.float32)
    ch = np.arange(128) // 64
    c["sg_m01"] = (ch[None, :] >= ch[:, None]).astype(np.float32)
    pp = np.arange(128)[:, None]
    ff = np.arange(128)[None, :]
    m0 = ~((pp < 64) & (ff >= 64))
    m4 = ~((pp >= 64) & (ff < 64))
    c["cb_m04"] = np.stack([m0, m4], axis=1).astype(np.float32)
    return c


def _core_inputs(layers, inp, b, half, xfull):
    base = 0 if half == 0 else HALO
    cst = _consts()
    m = {}
    if 0 in layers:
        xseq = np.zeros((KS, D), np.float32)
        if half == 0:
            xseq[HALO:] = xfull[:W]
        else:
            xseq[:] = xfull
        m["xseq"] = xseq
        tok = np.arange(KS).reshape(32, 128).T
        m["valid"] = ((tok + base - HALO) >= 0).astype(np.float32)
        m["sb_w_qkv"] = inp["sb_w_qkv"][0]
        m["sb_w_o"] = inp["sb_w_o"][0]
        m["sb_mask"] = cst["sb_mask"]
        m["sb_tri"] = cst["sb_tri"]
    else:
        m["xin"] = np.ascontiguousarray(xfull[base:base + W])
    m["cfm"] = np.ascontiguousarray(inp["c"][b].reshape(8, 128).T)
    m["ident"] = cst["ident"]
    m["ngfm"] = np.ascontiguousarray(inp["norm_g"].reshape(16, 8, 128).transpose(2, 0, 1).reshape(128, 128))
    m["ngrow"] = np.ascontiguousarray(inp["norm_g"].reshape(1, 16 * D))
    m["ada_w"] = inp["ada_w"][list(layers)]
    m["ada_b"] = np.ascontiguousarray(inp["ada_b"].reshape(1, -1))
    m["ffn_w_in"] = inp["ffn_w_in"][list(layers)]
    m["ffn_w_out"] = inp["ffn_w_out"][list(layers)]
    if 1 in layers:
        m["sg_w_in"] = inp["sg_w_in"][0]
        m["sg_lng"] = np.ascontiguousarray(inp["sg_ln_g"][0].reshape(1, 2048))
        m["sg_wsT"] = np.ascontiguousarray(inp["sg_w_s"][0].transpose(2, 0, 1))
        m["sg_m01"] = cst["sg_m01"]
        m["sg_bias"] = np.ascontiguousarray(inp["sg_bias"][0].reshape(1, 1024))
        m["sg_w_out"] = inp["sg_w_out"][0]
    if 2 in layers:
        m["sc_w_in"] = inp["sc_w_in"][0]
        m["sc_cw"] = np.ascontiguousarray(inp["sc_conv_w"][0].reshape(3, 8, 128).transpose(2, 0, 1).reshape(128, 24))
        m["sc_w_out"] = inp["sc_w_out"][0]
    if 3 in layers:
        m["cb_w_qkv"] = inp["cb_w_qkv"][0]
        m["cb_w_o"] = inp["cb_w_o"][0]
        rb = inp["cb_rel_bias"][0]
        p = np.arange(128)[:, None]
        f = np.arange(128)[None, :]
        idx3 = np.clip(f - p + 128, -128, 128) + 128
        idx4 = np.clip(f - p, -128, 128) + 128
        bt = np.stack([rb[:, idx3], rb[:, idx4]], axis=1)
        m["cb_bt"] = np.ascontiguousarray(bt.transpose(2, 0, 1, 3))
        m["cb_bfar"] = np.ascontiguousarray(np.broadcast_to(rb[:, 256][None, :], (128, 16)))
        m["cb_m04"] = cst["cb_m04"]
    return {k: np.ascontiguousarray(v, dtype=np.float32) for k, v in m.items()}


LAUNCH_PLAN = [[0, 1, 2, 3]]
_NC_CACHE = {}


def run_layers(layers, inp, x, batches=(0, 1, 2, 3)):
    key = tuple(layers)
    if key not in _NC_CACHE:
        _NC_CACHE[key] = build(list(layers))
    nc = _NC_CACHE[key]
    in_maps = []
    for b in batches:
        for half in (0, 1):
            in_maps.append(_core_inputs(layers, inp, b, half, x[b]))
    res = run_bass_kernel_spmd(nc, in_maps, core_ids=list(range(len(in_maps))))
    out = np.empty_like(x)
    for i, b in enumerate(batches):
        a = res.results[2 * i]["xout"]
        bb = res.results[2 * i + 1]["xout"]
        out[b, :2048] = a[:2048]
        out[b, 2048:] = bb[2048 - HALO:]
    return out, res


def kernel(**inputs):
    inp = {k: np.asarray(v, dtype=np.float32) for k, v in inputs.items()}
    x = inp["x"]
    for layers in LAUNCH_PLAN:
        x, _ = run_layers(layers, inp, x)
    return x.astype(np.float32)
```
